# Optimizing a Trainium2 kernel written in Bass

```python
import jax, jax.numpy as jnp
from jax import lax
import numpy as np

D_MODEL = 4096
BATCH = 2
SEQ = 8192
DEPTH = 2

F32 = jnp.float32

MEM_LEN = 256
MEM_HEADS = 4
MEM_DH = 128
MEM_WIDTH = MEM_HEADS * MEM_DH

GLA_HEADS = 4
GLA_DK = D_MODEL // 16
GLA_DV = D_MODEL // 8
GLA_LOWRANK = 16
GLA_TAU = 16.0
GLA_KWIDTH = GLA_HEADS * GLA_DK
GLA_WIDTH = GLA_HEADS * GLA_DV

HGRN_EXPAND = 128
HGRN_WIDTH = D_MODEL // 2
HGRN_HEADS = HGRN_WIDTH // HGRN_EXPAND
HGRN_DV = HGRN_WIDTH // HGRN_HEADS
HGRN_FWIDTH = HGRN_HEADS * HGRN_EXPAND

MLSTM_WIDTH = 2 * D_MODEL
MLSTM_HEADS = 4
MLSTM_DH = MLSTM_WIDTH // MLSTM_HEADS
MLSTM_CONV = 4
MLSTM_QKV_BLOCK = 4

LIN_CHUNK = 64
MLSTM_CHUNK = 128

ALPHA = (2 * DEPTH) ** 0.25
BETA = (8 * DEPTH) ** -0.25

N_EVEN = (DEPTH + 1) // 2
N_ODD = DEPTH // 2

EVEN_SPLITS = (GLA_KWIDTH, GLA_KWIDTH, GLA_WIDTH, GLA_WIDTH, GLA_LOWRANK,
               HGRN_FWIDTH, HGRN_FWIDTH, HGRN_WIDTH, HGRN_WIDTH,
               MEM_WIDTH, MEM_WIDTH)
ODD_SPLITS = (MLSTM_WIDTH, MLSTM_WIDTH, MEM_WIDTH, MEM_WIDTH)
EVEN_IN = sum(EVEN_SPLITS)
ODD_IN = sum(ODD_SPLITS)
EVEN_OUT = GLA_WIDTH + HGRN_WIDTH + MEM_WIDTH
ODD_OUT = MLSTM_WIDTH + MEM_WIDTH

kernel_name = "hybrid_gla_hgrn2_mlstm_deepnorm"


def split_cols(h, sizes):
    return jnp.split(h, np.cumsum(sizes)[:-1].tolist(), axis=-1)


def to_heads(a, n_heads):
    B, T, _ = a.shape
    return a.reshape(B, T, n_heads, -1).transpose(0, 2, 1, 3)


def merge_heads(a):
    B, H, T, d = a.shape
    return a.transpose(0, 2, 1, 3).reshape(B, T, H * d)


def layer_norm(x, g, b, eps=1e-5):
    xf = x.astype(F32)
    mu = jnp.mean(xf, axis=-1, keepdims=True)
    var = jnp.mean(jnp.square(xf - mu), axis=-1, keepdims=True)
    return ((xf - mu) * lax.rsqrt(var + eps) * g.astype(F32) + b.astype(F32)).astype(x.dtype)


def rms_norm(x, g, eps=1e-6):
    xf = x.astype(F32)
    return xf * lax.rsqrt(jnp.mean(jnp.square(xf), axis=-1, keepdims=True) + eps) * g.astype(F32)


def head_layer_norm(x, eps=1e-6):
    xf = x.astype(F32)
    mu = jnp.mean(xf, axis=-1, keepdims=True)
    var = jnp.mean(jnp.square(xf - mu), axis=-1, keepdims=True)
    return (xf - mu) * lax.rsqrt(var + eps)


def chunk_gated_linear_attention(q, k, v, log_f, chunk):
    B, H, T, dk = q.shape
    dv = v.shape[-1]
    n = T // chunk

    def to_chunks(a):
        return jnp.moveaxis(a.astype(F32).reshape(B, H, n, chunk, a.shape[-1]), 2, 0)

    causal = jnp.tril(jnp.ones((chunk, chunk), bool))

    def step(S, inp):
        qc, kc, vc, gc = inp
        b = jnp.cumsum(gc, axis=-2)
        o_inter = jnp.einsum('bhld,bhdv->bhlv', qc * jnp.exp(b), S)
        diff = b[:, :, :, None, :] - b[:, :, None, :, :]
        decay = jnp.exp(jnp.where(causal[:, :, None], diff, -jnp.inf))
        scores = jnp.einsum('bhid,bhjd,bhijd->bhij', qc, kc, decay)
        o_intra = jnp.einsum('bhij,bhjv->bhiv', scores, vc)
        b_last = b[:, :, -1, :]
        k_dec = kc * jnp.exp(b_last[:, :, None, :] - b)
        S = jnp.exp(b_last)[..., None] * S + jnp.einsum('bhjd,bhjv->bhdv', k_dec, vc)
        return S, o_inter + o_intra

    S0 = jnp.zeros((B, H, dk, dv), F32)
    _, o = lax.scan(step, S0, (to_chunks(q), to_chunks(k), to_chunks(v), to_chunks(log_f)))
    return jnp.moveaxis(o, 0, 2).reshape(B, H, T, dv)


def chunk_mlstm(q, k, v, i_pre, f_pre, chunk):
    B, H, T, dk = q.shape
    dv = v.shape[-1]
    n = T // chunk
    k = k.astype(F32) * (dk ** -0.5)
    log_f = jax.nn.log_sigmoid(f_pre.astype(F32))

    def to_chunks(a):
        return jnp.moveaxis(a.astype(F32).reshape(B, H, n, chunk, a.shape[-1]), 2, 0)

    def to_chunks_s(a):
        return jnp.moveaxis(a.astype(F32).reshape(B, H, n, chunk), 2, 0)

    causal = jnp.tril(jnp.ones((chunk, chunk), bool))

    def step(carry, inp):
        C, nv, m = carry
        qc, kc, vc, ic, gc = inp
        b = jnp.cumsum(gc, axis=-1)
        D = jnp.where(causal, b[..., :, None] - b[..., None, :] + ic[..., None, :], -jnp.inf)
        inter_log = b + m[..., None]
        m_i = jnp.maximum(inter_log, jnp.max(D, axis=-1))
        w_intra = jnp.exp(D - m_i[..., None])
        w_inter = jnp.exp(inter_log - m_i)
        scores = jnp.einsum('bhid,bhjd->bhij', qc, kc) * w_intra
        num = (jnp.einsum('bhij,bhjv->bhiv', scores, vc)
               + w_inter[..., None] * jnp.einsum('bhid,bhdv->bhiv', qc, C))
        den = jnp.sum(scores, axis=-1) + w_inter * jnp.einsum('bhid,bhd->bhi', qc, nv)
        h = num / jnp.maximum(jnp.abs(den), jnp.exp(-m_i))[..., None]
        b_last = b[..., -1]
        log_wj = b_last[..., None] - b + ic
        m_new = jnp.maximum(b_last + m, jnp.max(log_wj, axis=-1))
        wj = jnp.exp(log_wj - m_new[..., None])
        carry_dec = jnp.exp(b_last + m - m_new)
        C = carry_dec[..., None, None] * C + jnp.einsum('bhjd,bhjv->bhdv', kc * wj[..., None], vc)
        nv = carry_dec[..., None] * nv + jnp.einsum('bhjd,bhj->bhd', kc, wj)
        return (C, nv, m_new), h

    init = (jnp.zeros((B, H, dk, dv), F32), jnp.zeros((B, H, dk), F32), jnp.zeros((B, H), F32))
    _, h = lax.scan(step, init, (to_chunks(q), to_chunks(k), to_chunks(v),
                                 to_chunks_s(i_pre), to_chunks_s(log_f)))
    return jnp.moveaxis(h, 0, 2).reshape(B, H, T, dv)


def memory_attention(q_flat, mem, w_k, w_v):
    B, T, _ = q_flat.shape
    q = q_flat.reshape(B, T, MEM_HEADS, MEM_DH)
    k = (mem @ w_k).reshape(B, -1, MEM_HEADS, MEM_DH)
    v = (mem @ w_v).reshape(B, -1, MEM_HEADS, MEM_DH)
    s = jnp.einsum('bthd,bmhd->bhtm', q, k).astype(F32) * (MEM_DH ** -0.5)
    p = jax.nn.softmax(s, axis=-1)
    return jnp.einsum('bhtm,bmhd->bthd', p, v.astype(F32)).reshape(B, T, MEM_WIDTH)


def causal_conv(x, w, b):
    K = w.shape[0]
    T = x.shape[1]
    xp = jnp.pad(x, ((0, 0), (K - 1, 0), (0, 0)))
    return sum(xp[:, j:j + T] * w[j] for j in range(K)) + b


def block_diag_proj(a, w):
    B, T, _ = a.shape
    nb, bi, bo = w.shape
    return jnp.einsum('btni,nio->btno', a.reshape(B, T, nb, bi), w).reshape(B, T, nb * bo)


def even_layer(x, mem, layer, lb_logits, w_in, gla_w_a2, gla_b_a, gla_norm_g, hgrn_norm_g,
               mem_w_k, mem_w_v, w_out, ln_g, ln_b):
    h = x @ w_in
    gq, gk, gv, gg, ga, hq, hf, hi, hg, mq, mg = split_cols(h, EVEN_SPLITS)
    log_a = jax.nn.log_sigmoid((ga @ gla_w_a2 + gla_b_a).astype(F32)) / GLA_TAU
    o = chunk_gated_linear_attention(to_heads(gq, GLA_HEADS) * (GLA_DK ** -0.5),
                                     to_heads(gk, GLA_HEADS), to_heads(gv, GLA_HEADS),
                                     to_heads(log_a, GLA_HEADS), LIN_CHUNK)
    gla_out = merge_heads(rms_norm(o, gla_norm_g)) * jax.nn.silu(gg.astype(F32))
    lb = jnp.cumsum(jax.nn.softmax(lb_logits.astype(F32), axis=0), axis=0)[layer]
    f = lb + (1.0 - lb) * jax.nn.sigmoid(hf.astype(F32))
    o = chunk_gated_linear_attention(to_heads(jax.nn.silu(hq.astype(F32)), HGRN_HEADS),
                                     to_heads(1.0 - f, HGRN_HEADS), to_heads(hi, HGRN_HEADS),
                                     to_heads(jnp.log(f), HGRN_HEADS), LIN_CHUNK)
    hgrn_out = merge_heads(rms_norm(o, hgrn_norm_g)) * jax.nn.silu(hg.astype(F32))
    mem_out = memory_attention(mq, mem, mem_w_k, mem_w_v) * jax.nn.silu(mg.astype(F32))
    y = jnp.concatenate([gla_out, hgrn_out, mem_out], axis=-1).astype(x.dtype) @ w_out
    return layer_norm(ALPHA * x + y, ln_g, ln_b)


def odd_layer(x, mem, w_in, conv_w, conv_b, w_q, w_k, w_v, w_if, b_if, mh_norm_g, skip,
              mem_w_k, mem_w_v, w_out, ln_g, ln_b):
    h = x @ w_in
    xm, z, mq, mg = split_cols(h, ODD_SPLITS)
    xc = jax.nn.silu(causal_conv(xm, conv_w, conv_b))
    q = block_diag_proj(xc, w_q)
    k = block_diag_proj(xc, w_k)
    v = block_diag_proj(xm, w_v)
    gates = (jnp.concatenate([q, k, v], axis=-1) @ w_if + b_if).astype(F32)
    i_pre = gates[..., :MLSTM_HEADS].transpose(0, 2, 1)
    f_pre = gates[..., MLSTM_HEADS:].transpose(0, 2, 1)
    hh = chunk_mlstm(to_heads(q, MLSTM_HEADS), to_heads(k, MLSTM_HEADS), to_heads(v, MLSTM_HEADS),
                     i_pre, f_pre, MLSTM_CHUNK)
    hn = merge_heads(head_layer_norm(hh)) * mh_norm_g.astype(F32)
    mlstm_out = (hn + skip.astype(F32) * xc.astype(F32)) * jax.nn.silu(z.astype(F32))
    mem_out = memory_attention(mq, mem, mem_w_k, mem_w_v) * jax.nn.silu(mg.astype(F32))
    y = jnp.concatenate([mlstm_out, mem_out], axis=-1).astype(x.dtype) @ w_out
    return layer_norm(ALPHA * x + y, ln_g, ln_b)


def setup_inputs(seed: int = 0) -> dict:
    key = jax.random.key(seed)
    ks = iter(jax.random.split(key, 40))

    def nrm(shape, scale):
        return jax.random.normal(next(ks), shape, F32) * scale

    D = D_MODEL
    b_if_i = nrm((N_ODD, MLSTM_HEADS), 0.1)
    b_if_f = jnp.linspace(3.0, 6.0, MLSTM_HEADS, dtype=F32)[None] + nrm((N_ODD, MLSTM_HEADS), 0.1)
    return {
        "x": nrm((BATCH, SEQ, D), 1.0),
        "mem": nrm((BATCH, MEM_LEN, D), 1.0),
        "hgrn_lb_logits": nrm((DEPTH + 1, HGRN_FWIDTH), 0.1),
        "ev_w_in": nrm((N_EVEN, D, EVEN_IN), D ** -0.5),
        "ev_gla_w_a2": nrm((N_EVEN, GLA_LOWRANK, GLA_KWIDTH), GLA_LOWRANK ** -0.5),
        "ev_gla_b_a": nrm((N_EVEN, GLA_KWIDTH), 0.1),
        "ev_gla_norm_g": 1.0 + nrm((N_EVEN, GLA_DV), 0.02),
        "ev_hgrn_norm_g": 1.0 + nrm((N_EVEN, HGRN_DV), 0.02),
        "ev_mem_w_k": nrm((N_EVEN, D, MEM_WIDTH), D ** -0.5),
        "ev_mem_w_v": nrm((N_EVEN, D, MEM_WIDTH), BETA * D ** -0.5),
        "ev_w_out": nrm((N_EVEN, EVEN_OUT, D), BETA * EVEN_OUT ** -0.5),
        "ev_ln_g": 1.0 + nrm((N_EVEN, D), 0.02),
        "ev_ln_b": nrm((N_EVEN, D), 0.02),
        "od_w_in": nrm((N_ODD, D, ODD_IN), D ** -0.5),
        "od_conv_w": nrm((N_ODD, MLSTM_CONV, MLSTM_WIDTH), MLSTM_CONV ** -0.5),
        "od_conv_b": nrm((N_ODD, MLSTM_WIDTH), 0.02),
        "od_w_q": nrm((N_ODD, MLSTM_WIDTH // MLSTM_QKV_BLOCK, MLSTM_QKV_BLOCK, MLSTM_QKV_BLOCK), MLSTM_QKV_BLOCK ** -0.5),
        "od_w_k": nrm((N_ODD, MLSTM_WIDTH // MLSTM_QKV_BLOCK, MLSTM_QKV_BLOCK, MLSTM_QKV_BLOCK), MLSTM_QKV_BLOCK ** -0.5),
        "od_w_v": nrm((N_ODD, MLSTM_WIDTH // MLSTM_QKV_BLOCK, MLSTM_QKV_BLOCK, MLSTM_QKV_BLOCK), MLSTM_QKV_BLOCK ** -0.5),
        "od_w_if": nrm((N_ODD, 3 * MLSTM_WIDTH, 2 * MLSTM_HEADS), (3 * MLSTM_WIDTH) ** -0.5),
        "od_b_if": jnp.concatenate([b_if_i, b_if_f], axis=-1),
        "od_mh_norm_g": 1.0 + nrm((N_ODD, MLSTM_WIDTH), 0.02),
        "od_skip": 1.0 + nrm((N_ODD, MLSTM_WIDTH), 0.02),
        "od_mem_w_k": nrm((N_ODD, D, MEM_WIDTH), D ** -0.5),
        "od_mem_w_v": nrm((N_ODD, D, MEM_WIDTH), BETA * D ** -0.5),
        "od_w_out": nrm((N_ODD, ODD_OUT, D), BETA * ODD_OUT ** -0.5),
        "od_ln_g": 1.0 + nrm((N_ODD, D), 0.02),
        "od_ln_b": nrm((N_ODD, D), 0.02),
    }


def reference(x, mem, hgrn_lb_logits, ev_w_in, ev_gla_w_a2, ev_gla_b_a, ev_gla_norm_g, ev_hgrn_norm_g,
              ev_mem_w_k, ev_mem_w_v, ev_w_out, ev_ln_g, ev_ln_b, od_w_in, od_conv_w, od_conv_b,
              od_w_q, od_w_k, od_w_v, od_w_if, od_b_if, od_mh_norm_g, od_skip, od_mem_w_k,
              od_mem_w_v, od_w_out, od_ln_g, od_ln_b):
    for layer in range(DEPTH):
        i = layer // 2
        if layer % 2 == 0:
            x = even_layer(x, mem, layer, hgrn_lb_logits, ev_w_in[i], ev_gla_w_a2[i], ev_gla_b_a[i],
                           ev_gla_norm_g[i], ev_hgrn_norm_g[i], ev_mem_w_k[i], ev_mem_w_v[i],
                           ev_w_out[i], ev_ln_g[i], ev_ln_b[i])
        else:
            x = odd_layer(x, mem, od_w_in[i], od_conv_w[i], od_conv_b[i], od_w_q[i], od_w_k[i],
                          od_w_v[i], od_w_if[i], od_b_if[i], od_mh_norm_g[i], od_skip[i],
                          od_mem_w_k[i], od_mem_w_v[i], od_w_out[i], od_ln_g[i], od_ln_b[i])
    return x
```

```python
import contextlib
import numpy as np
import ml_dtypes
import concourse.bass as bass
import concourse.mybir as mybir
from concourse.bass_utils import run_bass_kernel_spmd

F32 = mybir.dt.float32
BF16 = mybir.dt.bfloat16
AF = mybir.ActivationFunctionType
ALU = mybir.AluOpType
AX = mybir.AxisListType

D = 4096
NCORES = 8
SEG_TOKENS = 2048
NSEM_SP = 16
ALPHA = 4 ** 0.25
GLA_TAU = 16.0


class View:
    __slots__ = ("ap", "buf")

    def __init__(self, ap, buf):
        self.ap = ap
        self.buf = buf

    def __getitem__(self, idx):
        return View(self.ap[idx], self.buf)

    def rearrange(self, pat, **kw):
        return View(self.ap.rearrange(pat, **kw), self.buf)


class Buf:
    __slots__ = ("t", "w", "r", "name")

    def __init__(self, t, name=""):
        self.t = t
        self.w = {}
        self.r = {}
        self.name = name

    def __getitem__(self, idx):
        return View(self.t[idx], self)

    def view(self, ap):
        return View(ap, self)


class Eng:
    def __init__(self, prog, name, h, nsem_dma=0):
        self.name = name
        self.h = h
        self.sem = prog.nc.alloc_semaphore("s_" + name)
        self.cnt = 0
        self.waited = {}
        self.dsems = [prog.nc.alloc_semaphore("d_%s%d" % (name, i)) for i in range(nsem_dma)]
        self.dcnt = [0] * nsem_dma
        self.dnext = 0


def _ap(x):
    return x.ap if isinstance(x, View) else x


class Prog:
    def __init__(self, nc):
        self.nc = nc
        self.pe = Eng(self, "pe", nc.tensor)
        self.act = Eng(self, "act", nc.scalar)
        self.dve = Eng(self, "dve", nc.vector)
        self.pool = Eng(self, "pool", nc.gpsimd, nsem_dma=8)
        self.sp = Eng(self, "sp", nc.sync, nsem_dma=NSEM_SP)
        self.n_ins = 0
        self._uid = 0
        self._flip = 0
        self.stacks = [contextlib.ExitStack()]
        self.all_sems = [e.sem for e in (self.pe, self.act, self.dve, self.pool, self.sp)] + self.pool.dsems + self.sp.dsems
        for sm_ in self.all_sems:
            nc.gpsimd.sem_clear(sm_)
        nc.all_engine_barrier()

    def _wait(self, e, deps):
        for sid, (sem, cnt) in deps.items():
            if e is self.pe and sem is self.pe.sem:
                continue
            if e.waited.get(sid, 0) < cnt:
                e.h.wait_ge(sem, cnt)
                e.waited[sid] = cnt

    @staticmethod
    def _merge(dst, src):
        for sid, (sem, cnt) in src.items():
            if sid not in dst or dst[sid][1] < cnt:
                dst[sid] = (sem, cnt)

    def _deps(self, reads, writes, acc=False):
        deps = {}
        for b in reads:
            self._merge(deps, b.w)
        for b in writes:
            if not acc:
                self._merge(deps, b.w)
            self._merge(deps, b.r)
        return deps

    def _commit(self, st, reads, writes, acc):
        for b in writes:
            if acc:
                self._merge(b.w, st)
            else:
                b.w = dict(st)
                b.r = {}
        for b in reads:
            self._merge(b.r, st)
        self.n_ins += 1

    def op(self, e, fn, reads=(), writes=(), acc=False):
        reads = [v.buf for v in reads if isinstance(v, View)]
        writes = [v.buf for v in writes if isinstance(v, View)]
        self._wait(e, self._deps(reads, writes, acc))
        ins = fn(e.h)
        e.cnt += 1
        ins.then_inc(e.sem, 1)
        self._commit({id(e.sem): (e.sem, e.cnt)}, reads, writes, acc)
        return ins

    def dma(self, e, out, in_, acc=False, **kw):
        reads = [in_.buf]
        writes = [out.buf]
        k = e.dnext
        e.dnext = (k + 1) % len(e.dsems)
        sem = e.dsems[k]
        deps = self._deps(reads, writes, acc)
        if e.dcnt[k] > 0:
            deps[id(sem)] = (sem, e.dcnt[k])
        self._wait(e, deps)
        ins = e.h.dma_start(out=out.ap, in_=in_.ap, **kw)
        e.dcnt[k] += 16
        ins.then_inc(sem, 16)
        self._commit({id(sem): (sem, e.dcnt[k])}, reads, writes, acc)
        return ins

    def pad(self):
        import os
        scr = self.sb("padscr", [128, 8], F32)
        for nm, e in (("DVE", self.dve), ("ACT", self.act), ("POOL", self.pool)):
            for _ in range(int(os.environ.get("PAD_" + nm, "0"))):
                self.MEMSET(e, scr[:], 0.0) if e is not self.act else self.ACT(scr[:], scr[:], AF.Copy)
        for _ in range(int(os.environ.get("PAD_SP", "0"))):
            self.sp.h.wait_ge(self.sp.sem, 0)
        for _ in range(int(os.environ.get("PAD_PE", "0"))):
            self.pe.h.wait_ge(self.pe.sem, 0)

    def finish(self):
        self._finish_body()
        self.barrier()
        self.nc.all_engine_barrier()
        for sm_ in self.all_sems:
            self.nc.gpsimd.sem_clear(sm_)
        self.nc.all_engine_barrier()

    def _finish_body(self):
        self.pad()
        import os
        if os.environ.get("TAIL", "0") == "1":
            self.barrier()
            scr = self.sb("tailscr", [128, 64], F32)
            for _ in range(8):
                self.MEMSET(self.dve, scr[:], 0.0)
                self.ACT(scr[:], scr[:], AF.Copy)
                self.MEMSET(self.pool, scr[:], 0.0)
            self.barrier()
        for e in (self.sp, self.pool):
            for k, sem in enumerate(e.dsems):
                if e.dcnt[k] > 0:
                    self._wait(self.sp, {id(sem): (sem, e.dcnt[k])})
        for e in (self.pe, self.act, self.dve, self.pool):
            if e.cnt > 0:
                self._wait(self.sp, {id(e.sem): (e.sem, e.cnt)})

    def name(self, base):
        self._uid += 1
        return "%s_%d" % (base, self._uid)

    def sb(self, name, shape, dt):
        n = self.name(name)
        return Buf(self.stacks[-1].enter_context(self.nc.sbuf_tensor(n, list(shape), dt)), n)

    def ps(self, name, shape, dt=F32):
        n = self.name(name)
        return Buf(self.stacks[-1].enter_context(self.nc.psum_tensor(n, list(shape), dt)), n)

    def barrier(self):
        engs = (self.pe, self.act, self.dve, self.pool, self.sp)
        for e in engs:
            deps = {}
            for o in engs:
                if o is not e and o.cnt > 0:
                    deps[id(o.sem)] = (o.sem, o.cnt)
                for k, sem in enumerate(o.dsems):
                    if o.dcnt[k] > 0:
                        deps[id(sem)] = (sem, o.dcnt[k])
            self._wait(e, deps)

    @contextlib.contextmanager
    def scope(self):
        st = contextlib.ExitStack()
        self.stacks.append(st)
        try:
            yield
        finally:
            self.barrier()
            self.stacks.pop()
            st.close()

    def dram(self, name, shape, dt, kind="Internal"):
        return Buf(self.nc.dram_tensor(name, list(shape), dt, kind=kind).ap(), name)

    def ACT(self, out, in_, func, bias=0.0, scale=1.0, accum=None, acc=False):
        rd = [in_, bias, scale]
        wr = [out] + ([accum] if accum is not None else [])
        kw = {}
        if accum is not None:
            kw["accum_out"] = accum.ap
        if not (isinstance(bias, float) and bias == 0.0):
            kw["bias"] = _ap(bias)
        if not (isinstance(scale, float) and scale == 1.0):
            kw["scale"] = _ap(scale)
        return self.op(self.act, lambda h: h.activation(out=out.ap, in_=in_.ap, func=func, **kw), rd, wr, acc=acc)

    def TS(self, e, out, in0, s1, s2, op0, op1=None, accum=None, acc=False):
        kw = {}
        if op1 is not None:
            kw["op1"] = op1
        if accum is not None:
            kw["accum_out"] = accum.ap
        wr = [out] + ([accum] if accum is not None else [])
        return self.op(e, lambda h: h.tensor_scalar(out=out.ap, in0=in0.ap, scalar1=_ap(s1), scalar2=_ap(s2), op0=op0, **kw),
                       [in0, s1, s2], wr, acc=acc)

    def TT(self, e, out, in0, in1, op, acc=False):
        return self.op(e, lambda h: h.tensor_tensor(out=out.ap, in0=in0.ap, in1=in1.ap, op=op), [in0, in1], [out], acc=acc)

    def STT(self, e, out, in0, scalar, in1, op0, op1, accum=None, acc=False):
        kw = {}
        if accum is not None:
            kw["accum_out"] = accum.ap
        wr = [out] + ([accum] if accum is not None else [])
        return self.op(e, lambda h: h.scalar_tensor_tensor(out=out.ap, in0=in0.ap, scalar=_ap(scalar), in1=in1.ap, op0=op0,
                                                           op1=op1, **kw), [in0, scalar, in1], wr, acc=acc)

    def CP(self, e, out, in_, acc=False):
        if e is self.act:
            return self.ACT(out, in_, AF.Copy, acc=acc)
        return self.op(e, lambda h: h.tensor_copy(out=out.ap, in_=in_.ap), [in_], [out], acc=acc)

    def anycp(self, out, in_, acc=False):
        self._flip ^= 1
        return self.CP(self.act if self._flip else self.dve, out, in_, acc=acc)

    def MM(self, out, lhsT, rhs, start=True, stop=True):
        return self.op(self.pe, lambda h: h.matmul(out.ap, lhsT=lhsT.ap, rhs=rhs.ap, start=start, stop=stop),
                       [lhsT, rhs], [out], acc=not start)

    def TR(self, out, in_, ident):
        return self.op(self.pe, lambda h: h.transpose(out.ap, in_.ap, ident.ap), [in_, ident], [out])

    def RED(self, e, out, in_, op, negate=False):
        return self.op(e, lambda h: h.tensor_reduce(out=out.ap, in_=in_.ap, axis=AX.X, op=op, negate=negate), [in_], [out])

    def RECIP(self, out, in_):
        return self.op(self.dve, lambda h: h.reciprocal(out=out.ap, in_=in_.ap), [in_], [out])

    def MEMSET(self, e, out, val):
        return self.op(e, lambda h: h.memset(out.ap, val), [], [out])

    def ASEL(self, out, pattern, cmp, fill, base, cm):
        return self.op(self.pool, lambda h: h.affine_select(out=out.ap, in_=out.ap, pattern=pattern, compare_op=cmp,
                                                            fill=fill, base=base, channel_multiplier=cm), [out], [out])


class Rot:
    def __init__(self, mk, n):
        self.b = [mk(i) for i in range(n)]
        self.i = 0

    def get(self):
        b = self.b[self.i]
        self.i = (self.i + 1) % len(self.b)
        return b


class Consts:
    def __init__(self, P):
        self.P = P
        mk = lambda n, dt: P.sb(n, [128, 128], dt)
        self.ident_f = mk("ident_f", F32)
        self.ident_b = mk("ident_b", BF16)
        self.U = mk("U", F32)
        self.SU = mk("SU", F32)
        self.ONES = mk("ONES", F32)
        self.MASKT = mk("MASKT", F32)
        self.NEGM = mk("NEGM", F32)
        self.ELAST = mk("ELAST", F32)
        for t in (self.ident_f, self.ident_b):
            P.MEMSET(P.pool, t[:], 1.0)
            P.ASEL(t[:], [[-1, 128]], ALU.is_equal, 0.0, 0, 1)
        for t in (self.U, self.MASKT):
            P.MEMSET(P.pool, t[:], 1.0)
            P.ASEL(t[:], [[1, 128]], ALU.is_ge, 0.0, 0, -1)
        P.MEMSET(P.pool, self.SU[:], 1.0)
        P.ASEL(self.SU[:], [[-1, 128]], ALU.is_gt, 0.0, 0, 1)
        P.MEMSET(P.pool, self.ONES[:], 1.0)
        P.MEMSET(P.pool, self.NEGM[:], 0.0)
        P.ASEL(self.NEGM[:], [[-1, 128]], ALU.is_ge, -1e30, 0, 1)
        self.U_b = mk("U_b", BF16)
        self.SU_b = mk("SU_b", BF16)
        self.ONES_b = mk("ONES_b", BF16)
        self.ELAST_b = mk("ELAST_b", BF16)
        P.MEMSET(P.pool, self.ELAST[:], 1.0)
        P.ASEL(self.ELAST[:], [[0, 128]], ALU.is_equal, 0.0, -127, 1)
        for src, dst in ((self.U, self.U_b), (self.SU, self.SU_b), (self.ONES, self.ONES_b), (self.ELAST, self.ELAST_b)):
            P.CP(P.pool, dst[:], src[:])


def cast_dram(P, src, dst, rows, cols):
    sv = src.t if len(src.t.shape) == 2 else src.t.rearrange("a p k n -> (a p) (k n)")
    dv = dst.t if len(dst.t.shape) == 2 else dst.t.rearrange("a p k n -> (a p) (k n)")
    cstep = min(cols, 8192)
    for r in range(0, rows, 128):
        rr = min(128, rows - r)
        for c in range(0, cols, cstep):
            ce = min(c + cstep, cols)
            P.dma(P.pool, dst.view(dv[r:r + rr, c:ce]), src.view(sv[r:r + rr, c:ce]), acc=True)


def gemm_fm(P, aT, K, T, wl, nch, outT, ncols, psum, TT=512):
    KC = K // 128
    TT = min(TT, T)
    xts = Rot(lambda i: P.sb("g_xt", [128, KC, TT], BF16), 2)
    wts = Rot(lambda i: P.sb("g_wt", [128, KC, 128], BF16), 3)
    sts = Rot(lambda i: P.sb("g_st", [128, TT], F32), 3)
    aTv = aT.t.rearrange("(kc p) t -> p kc t", p=128)
    for tt in range(T // TT):
        xt = xts.get()
        half = KC // 2
        P.dma(P.sp, xt[:, 0:half, :], aT.view(aTv[:, 0:half, tt * TT:(tt + 1) * TT]), acc=True)
        P.dma(P.sp, xt[:, half:KC, :], aT.view(aTv[:, half:KC, tt * TT:(tt + 1) * TT]), acc=True)
        for n in range(nch):
            cols = min(128, ncols - n * 128)
            wt = wts.get()
            P.dma(P.sp, wt[:], wl[n])
            ps = psum.get()
            for kc in range(KC):
                P.MM(ps[0:cols, 0:TT], wt[:, kc, 0:cols], xt[:, kc, :], start=(kc == 0), stop=(kc == KC - 1))
            st = sts.get()
            P.anycp(st[0:cols, :], ps[0:cols, 0:TT])
            P.dma(P.sp, outT[n * 128:n * 128 + cols, tt * TT:(tt + 1) * TT], st[0:cols, :], acc=True)


def gemm_tm(P, aT, K, T, wl, nblk, out, psum, odt=F32):
    KC = K // 128
    TT = min(512, T)
    NB = TT // 128
    xts = Rot(lambda i: P.sb("t_xt", [128, KC, TT], BF16), 2)
    wts = Rot(lambda i: P.sb("t_wt", [128, KC // 4, 512], BF16), 3)
    sts = Rot(lambda i: P.sb("t_st", [128, 512], odt), 3)
    aTv = aT.t.rearrange("(kc p) t -> p kc t", p=128)
    KG = KC // 4
    for tt in range(T // TT):
        xt = xts.get()
        half = KC // 2
        P.dma(P.sp, xt[:, 0:half, :], aT.view(aTv[:, 0:half, tt * TT:(tt + 1) * TT]), acc=True)
        P.dma(P.sp, xt[:, half:KC, :], aT.view(aTv[:, half:KC, tt * TT:(tt + 1) * TT]), acc=True)
        for n in range(nblk):
            pss = [psum.get() for _ in range(NB)]
            for kg in range(4):
                wt = wts.get()
                P.dma(P.sp, wt[:], wl[n, :, kg * KG:(kg + 1) * KG, :])
                for b in range(NB):
                    for k2 in range(KG):
                        kc = kg * KG + k2
                        P.MM(pss[b][:, :], xt[:, kc, b * 128:(b + 1) * 128], wt[:, k2, :], start=(kc == 0), stop=(kc == KC - 1))
            for b in range(NB):
                st = sts.get()
                P.anycp(st[:], pss[b][:, :])
                t0 = tt * TT + b * 128
                P.dma(P.sp, out[t0:t0 + 128, n * 512:(n + 1) * 512], st[:], acc=True)


NFM0 = 2832
NCH0 = 23


def split_bf16(P, src, pieces, tmps):
    cur = src
    for i, pc in enumerate(pieces):
        P.CP(P.act, pc, cur)
        if i + 1 < len(pieces):
            P.TT(P.dve, tmps[i], cur, pc, ALU.subtract)
            cur = tmps[i]


def silu_gate(P, out, x, tmp):
    P.ACT(tmp, x, AF.Exp, scale=-1.0)
    P.TS(P.dve, tmp, tmp, 1.0, None, ALU.add)
    P.RECIP(tmp, tmp)
    P.TT(P.dve, out, x, tmp, ALU.mult)


def mem_prep(P, C, memT, wk, wv, psum):
    KC = D // 128
    mt = P.sb("memT", [128, KC, 256], BF16)
    mv = memT.t.rearrange("(kc p) m -> p kc m", p=128)
    P.dma(P.pool, mt[:, 0:16, :], memT.view(mv[:, 0:16, :]), acc=True)
    P.dma(P.pool, mt[:, 16:32, :], memT.view(mv[:, 16:32, :]), acc=True)
    wkt = P.sb("wkt", [128, KC, 128], BF16)
    wvt = P.sb("wvt", [128, KC, 128], BF16)
    P.dma(P.pool, wkt[:], wk[:])
    P.dma(P.pool, wvt[:], wv[:])
    KT = P.sb("memKT", [128, 256], BF16)
    V = P.sb("memV", [128, 2, 128], BF16)
    ps = psum.get()
    for kc in range(KC):
        P.MM(ps[:, 0:256], wkt[:, kc, :], mt[:, kc, :], start=(kc == 0), stop=(kc == KC - 1))
    P.CP(P.dve, KT[:], ps[:, 0:256])
    for mc in range(2):
        ps = psum.get()
        for kc in range(KC):
            P.MM(ps[:, 0:128], mt[:, kc, mc * 128:(mc + 1) * 128], wvt[:, kc, :], start=(kc == 0), stop=(kc == KC - 1))
        P.CP(P.dve, V[:, mc, :], ps[:, 0:128])
    return KT, V


def mem_attn_block(P, C, KT, V, mqT, mgT, outrows, t0, psum, W):
    q32 = W["ma_q32"].get()
    g32 = W["ma_g32"].get()
    P.dma(P.sp, q32[:], mqT[:, t0:t0 + 128])
    P.dma(P.sp, g32[:], mgT[:, t0:t0 + 128])
    qb = W["ma_qb"].get()
    P.CP(P.dve, qb[:], q32[:])
    ps = psum.get()
    P.MM(ps[:, 0:256], qb[:], KT[:])
    nmx = W["ma_v1"].get()
    P.RED(P.dve, nmx[:], ps[:, 0:256], ALU.max, negate=True)
    sc = 128 ** -0.5
    P.TS(P.dve, nmx[:], nmx[:], sc, None, ALU.mult)
    e = W["ma_e"].get()
    rs = W["ma_v2"].get()
    P.ACT(e[:], ps[:, 0:256], AF.Exp, bias=nmx[:], scale=sc, accum=rs[:])
    P.RECIP(rs[:], rs[:])
    pb = W["ma_pb"].get()
    P.TS(P.dve, pb[:], e[:], rs[:], None, ALU.mult)
    pT = W["ma_pT"].get()
    for mc in range(2):
        pst = W["psb"].get()
        P.TR(pst[:, 0:128], pb[:, mc * 128:(mc + 1) * 128], C.ident_b[:])
        P.anycp(pT[:, mc, :], pst[:, 0:128])
    ps2 = psum.get()
    for mc in range(2):
        P.MM(ps2[:, 0:128], V[:, mc, :], pT[:, mc, :], start=(mc == 0), stop=(mc == 1))
    gate = W["ma_gate"].get()
    silu_gate(P, gate[:], g32[:], W["ma_tmp"].get()[:])
    ob = W["ma_ob"].get()
    P.TT(P.dve, ob[:], ps2[:, 0:128], gate[:], ALU.mult)
    P.dma(P.sp, outrows[:, t0:t0 + 128], ob[:])


def lin_attn_core(P, C, psum, W, qtil, ktil, kdec, v, elast, emid, S32, Sbf, NDK, DV):
    for dc in range(NDK):
        P.TS(P.dve, Sbf[:, dc, :], S32[:, dc, :], emid[:, dc:dc + 1], None, ALU.mult)
    psA = psum.get()
    for dc in range(NDK):
        P.MM(psA[:, 0:128], ktil[:, dc, :], qtil[:, dc, :], start=(dc == 0), stop=(dc == NDK - 1))
    AT = W["AT"].get()
    P.TT(P.dve, AT[:], psA[:, 0:128], C.MASKT[:], ALU.mult)
    pso = psum.get()
    P.MM(pso[:, 0:DV], AT[:], v, start=True, stop=False)
    for dc in range(NDK):
        P.MM(pso[:, 0:DV], qtil[:, dc, :], Sbf[:, dc, :], start=False, stop=(dc == NDK - 1))
    for dc in range(NDK):
        psS = psum.get()
        P.MM(psS[:, 0:DV], kdec[:, dc * 128:(dc + 1) * 128], v)
        P.STT(P.dve, S32[:, dc, :], S32[:, dc, :], elast[:, dc:dc + 1], psS[:, 0:DV], ALU.mult, ALU.add)
    return pso


def rms_out(P, C, W, pso, DV, norm_g, gT_dram, outrows, t0, tag):
    nchunk = DV // 128
    junk = W[tag + "junk"].get()
    ss = W[tag + "v1"].get()
    o32 = W[tag + "o32"].get()
    P.CP(P.act, o32[:, 0:DV], pso[:, 0:DV])
    P.STT(P.dve, junk[:, 0:DV], o32[:, 0:DV], 1.0, o32[:, 0:DV], ALU.mult, ALU.mult, accum=ss[:])
    P.ACT(ss[:], ss[:], AF.Ln, bias=C.eps6[:], scale=1.0 / DV)
    P.ACT(ss[:], ss[:], AF.Exp, scale=-0.5)
    on = W[tag + "on"].get()
    P.TS(P.dve, on[:, 0:DV], o32[:, 0:DV], ss[:], None, ALU.mult)
    g32 = W[tag + "g32"].get()
    P.dma(P.sp, g32[:, 0:nchunk, :], gT_dram[:, :, t0:t0 + 128])
    gate = W[tag + "gate"].get()
    silu_gate(P, gate[:, 0:nchunk, :], g32[:, 0:nchunk, :], W[tag + "gtmp"].get()[:, 0:nchunk, :])
    pst = W["psb"].get()
    ob = W[tag + "ob"].get()
    for c in range(nchunk):
        P.TR(pst[:, c * 128:(c + 1) * 128], on[:, c * 128:(c + 1) * 128], C.ident_b[:])
    for c in range(nchunk):
        P.STT(P.dve, ob[:, c, :], pst[:, c * 128:(c + 1) * 128], norm_g[:, c:c + 1], gate[:, c, :], ALU.mult, ALU.mult)
    P.dma(P.sp, outrows[:, 0:nchunk, t0:t0 + 128], ob[:, 0:nchunk, :])


def phase_l0a(P, C, T, io, psum, stop_after=None):
    hT = io["hT"]
    hv = io["hv"]
    cast_dram(P, io["xT"], io["xTb"], D, T)
    cast_dram(P, io["w_fm"], io["w_fmb"], NCH0 * 128, 4096)
    cast_dram(P, io["w_tm"], io["w_tmb"], 2 * 128, 32 * 512)
    with P.scope():
        gemm_fm(P, io["xTb"], D, T, io["w_fmb"], NCH0, hT, NFM0, psum)
    if stop_after == "fm":
        return
    with P.scope():
        gemm_tm(P, io["xTb"], D, T, io["w_tmb"], 2, hv, psum, odt=BF16)
    if stop_after == "tm":
        return

    wa2 = P.sb("wa2", [16, 256], F32)
    ba = P.sb("ba", [1, 256], F32)
    P.dma(P.sp, wa2[:], io["w_a2"][:, :])
    P.dma(P.sp, ba[:], io["b_a"][:, :])
    ones1 = P.sb("ones1", [1, 128], F32)
    P.MEMSET(P.dve, ones1[:], 1.0)
    C.eps6 = P.sb("eps6", [128, 1], F32)
    P.MEMSET(P.dve, C.eps6[:], 1e-6)
    gng = P.sb("gng", [128, 4], F32)
    hng = P.sb("hng", [128, 1], F32)
    P.dma(P.sp, gng[:], io["gla_ng"][:, :])
    P.dma(P.sp, hng[:], io["hgrn_ng"][:, :])
    lg = P.sb("lg", [128, 3, 4], F32)
    P.dma(P.sp, lg[:], io["lb_logits"][:, :, :])
    P.ACT(lg[:], lg[:], AF.Exp)
    lsum = P.sb("lsum", [128, 4], F32)
    P.TT(P.dve, lsum[:], lg[:, 0, :], lg[:, 1, :], ALU.add)
    P.TT(P.dve, lsum[:], lsum[:], lg[:, 2, :], ALU.add)
    P.RECIP(lsum[:], lsum[:])
    lb = P.sb("lb", [128, 4], F32)
    P.TT(P.dve, lb[:], lg[:, 0, :], lsum[:], ALU.mult)
    omlb = P.sb("omlb", [128, 4], F32)
    P.TS(P.dve, omlb[:], lb[:], -1.0, 1.0, ALU.mult, ALU.add)

    KT, V = mem_prep(P, C, io["memT"], io["w_mk"], io["w_mv"], psum)

    W = {}
    rot = lambda name, shape, dt, n=2: Rot(lambda i: P.sb(name, shape, dt), n)
    W["psb"] = C.psb
    for nm, shape, dt in [("ma_q32", [128, 128], F32), ("ma_g32", [128, 128], F32), ("ma_qb", [128, 128], BF16),
                          ("ma_v1", [128, 1], F32), ("ma_v2", [128, 1], F32), ("ma_e", [128, 256], F32),
                          ("ma_pb", [128, 256], BF16), ("ma_pT", [128, 2, 128], BF16), ("ma_gate", [128, 128], F32),
                          ("ma_tmp", [128, 128], F32), ("ma_ob", [128, 128], BF16), ("AT", [128, 128], BF16)]:
        W[nm] = rot(nm, shape, dt)
    for tag, DV in (("g", 512), ("h", 128)):
        nch = DV // 128
        W[tag + "junk"] = rot(tag + "junk", [128, DV], F32, 1)
        W[tag + "o32"] = rot(tag + "o32", [128, DV], F32)
        W[tag + "v1"] = rot(tag + "v1", [128, 1], F32)
        W[tag + "on"] = rot(tag + "on", [128, DV], BF16)
        W[tag + "g32"] = rot(tag + "g32", [128, nch, 128], F32)
        W[tag + "gate"] = rot(tag + "gate", [128, nch, 128], F32)
        W[tag + "gtmp"] = rot(tag + "gtmp", [128, nch, 128], F32, 1)
        W[tag + "ob"] = rot(tag + "ob", [128, nch, 128], BF16)

    gS32 = P.sb("gS32", [128, 2, 512], F32)
    gSbf = P.sb("gSbf", [128, 2, 512], BF16)
    hS32 = [P.sb("hS32", [128, 1, 128], F32) for _ in range(4)]
    hSbf = [P.sb("hSbf", [128, 1, 128], BF16) for _ in range(4)]
    for t in [gSbf] + hSbf:
        P.MEMSET(P.pool, t[:], 0.0)
    P.dma(P.pool, gS32[:], io["gS_in"][:, :, :])
    for hd in range(4):
        P.dma(P.pool, hS32[hd][:], io["hS_in"][hd])

    wa2b = P.sb("wa2b", [16, 256], BF16)
    bab = P.sb("bab", [1, 256], BF16)
    ones1b = P.sb("ones1b", [1, 128], BF16)
    P.CP(P.dve, wa2b[:], wa2[:]); P.CP(P.dve, bab[:], ba[:]); P.CP(P.dve, ones1b[:], ones1[:])
    r_gab = rot("gab", [16, 128], BF16)
    r_lp = [rot("glp%d" % i, [128, 256], BF16) for i in range(3)]
    r_ltmp = [rot("gltmp%d" % i, [128, 256], F32, 1) for i in range(2)]
    r_kp = [rot("gkp%d" % i, [128, 2, 128], BF16) for i in range(2)]
    r_ktmp = rot("gktmp", [128, 2, 128], F32, 1)
    r_gq = rot("gq32", [128, 2, 128], F32)
    r_gk = rot("gk32", [128, 2, 128], F32)
    r_ga = rot("ga32", [16, 128], F32)
    r_gv = rot("gvb", [128, 512], BF16)
    r_l = rot("gl", [128, 256], F32)
    r_eb = rot("geb", [128, 2, 128], F32)
    r_ei = rot("gei", [128, 2, 128], F32)
    r_bp = rot("gbp", [128, 2], F32)
    r_bn = rot("gbn", [128, 2], F32)
    r_el = rot("gel", [128, 2], F32)
    r_em = rot("gem", [128, 2], F32)
    r_hem = rot("hem", [128, 4], F32)
    r_stmp = rot("hstmp", [128, 4, 128], F32, 1)
    r_qt = rot("gqt", [128, 2, 128], BF16)
    r_kt = rot("gkt", [128, 2, 128], BF16)
    r_ed = rot("ged", [128, 256], F32)
    r_kd = rot("gkd", [128, 256], BF16)
    r_gbp = [rot("hgbp%d" % i, [128, 4, 128], BF16) for i in range(3)]
    r_gbt = [rot("hgbt%d" % i, [128, 4, 128], F32, 1) for i in range(2)]
    r_gtp = [rot("hgtp%d" % i, [128, 4, 128], BF16) for i in range(3)]
    r_op = [rot("hop%d" % i, [128, 4, 128], BF16) for i in range(2)]
    r_hq = rot("hq32", [128, 4, 128], F32)
    r_hf = rot("hf32", [128, 4, 128], F32)
    r_hv = rot("hvb", [128, 512], BF16)
    r_t1 = rot("ht1", [128, 4, 128], F32)
    r_fT = rot("hfT", [128, 4, 128], F32)
    r_gT = rot("hgT", [128, 4, 128], F32)
    r_gtok = rot("hgtok", [128, 4, 128], F32)
    r_heb = rot("heb", [128, 4, 128], F32)
    r_hei = rot("hei", [128, 4, 128], F32)
    r_hbp = rot("hbp", [128, 4], F32)
    r_hbn = rot("hbn", [128, 4], F32)
    r_hel = rot("hel", [128, 4], F32)
    r_hqt = rot("hqt", [128, 4, 128], BF16)
    r_hkt = rot("hkt", [128, 4, 128], BF16)
    r_hed = rot("hed", [128, 4, 128], F32)
    r_hft = rot("hft", [128, 4, 128], F32)
    r_hkd = rot("hkd", [128, 4, 128], BF16)

    hTt = hT.t
    gqT = hT.view(hTt[0:256, :].rearrange("(c p) t -> p c t", p=128))
    gkT = hT.view(hTt[256:512, :].rearrange("(c p) t -> p c t", p=128))
    ggT = hT.view(hTt[512:1024, :].rearrange("(c p) t -> p c t", p=128))
    hqT = hT.view(hTt[1024:1536, :].rearrange("(c p) t -> p c t", p=128))
    hfT = hT.view(hTt[1536:2048, :].rearrange("(c p) t -> p c t", p=128))
    hgT = hT.view(hTt[2048:2560, :].rearrange("(c p) t -> p c t", p=128))
    mqT = hT[2560:2688, :]
    mgT = hT[2688:2816, :]
    gaT = hT[2816:2832, :]
    outT = io["outT"]
    out_g = outT.view(outT.t[0:512, :].rearrange("(c p) t -> p c t", p=128))
    out_h = [outT.view(outT.t[512 + 128 * hd:640 + 128 * hd, :].rearrange("(c p) t -> p c t", p=128)) for hd in range(4)]
    out_m = outT[1024:1152, :]

    if stop_after == "prep":
        return
    for ck in range(T // 128):
        t0 = ck * 128
        mem_attn_block(P, C, KT, V, mqT, mgT, out_m, t0, psum, W)
        if stop_after == "mem":
            continue

        gq = r_gq.get(); gk = r_gk.get(); ga = r_ga.get(); gv = r_gv.get()
        P.dma(P.sp, gq[:], gqT[:, :, t0:t0 + 128])
        P.dma(P.sp, gk[:], gkT[:, :, t0:t0 + 128])
        P.dma(P.sp, ga[:], gaT[:, t0:t0 + 128])
        P.dma(P.sp, gv[:], hv[t0:t0 + 128, 0:512])
        gab = r_gab.get()
        P.CP(P.act, gab[:], ga[:])
        psz = psum.get()
        P.MM(psz[:, 0:256], gab[:], wa2b[:], start=True, stop=False)
        P.MM(psz[:, 0:256], ones1b[:], bab[:], start=False, stop=True)
        l = r_l.get()
        P.ACT(l[:], psz[:, 0:256], AF.Exp, scale=-1.0)
        P.ACT(l[:], l[:], AF.Ln, bias=1.0)
        lp = [r.get() for r in r_lp]
        lt = [r.get() for r in r_ltmp]
        split_bf16(P, l[:], [x[:] for x in lp], [x[:] for x in lt])
        eb = r_eb.get(); ei = r_ei.get(); bp = r_bp.get(); bn = r_bn.get(); el = r_el.get(); em = r_em.get()
        qt = r_qt.get(); kt = r_kt.get()
        for dc in range(2):
            psb_ = psum.get()
            for p_ in range(3):
                P.MM(psb_[:, 0:128], lp[p_][:, dc * 128:(dc + 1) * 128], C.U_b[:], start=(p_ == 0), stop=(p_ == 2))
            P.TS(P.dve, bp[:, dc:dc + 1], psb_[:, 63:64], 1.0 / GLA_TAU, None, ALU.mult)
            P.TS(P.dve, bn[:, dc:dc + 1], psb_[:, 63:64], -1.0 / GLA_TAU, None, ALU.mult)
            P.ACT(eb[:, dc, :], psb_[:, 0:128], AF.Exp, bias=bp[:, dc:dc + 1], scale=-1.0 / GLA_TAU)
            P.ACT(ei[:, dc, :], psb_[:, 0:128], AF.Exp, bias=bn[:, dc:dc + 1], scale=1.0 / GLA_TAU)
            P.ACT(el[:, dc:dc + 1], psb_[:, 127:128], AF.Exp, scale=-1.0 / GLA_TAU)
            P.ACT(em[:, dc:dc + 1], psb_[:, 63:64], AF.Exp, scale=-1.0 / GLA_TAU)
            P.STT(P.dve, qt[:, dc, :], gq[:, dc, :], 256 ** -0.5, eb[:, dc, :], ALU.mult, ALU.mult)
            P.TT(P.dve, kt[:, dc, :], gk[:, dc, :], ei[:, dc, :], ALU.mult)
        psr = psum.get()
        for p_ in range(3):
            P.MM(psr[:, 0:256], C.SU_b[:], lp[p_][:], start=(p_ == 0), stop=(p_ == 2))
        ed = r_ed.get()
        P.ACT(ed[:], psr[:, 0:256], AF.Exp, scale=-1.0 / GLA_TAU)
        kp = [r.get() for r in r_kp]
        ktmp = r_ktmp.get()
        split_bf16(P, gk[:], [x[:] for x in kp], [ktmp[:]])
        pskt = psum.get()
        for dc in range(2):
            for p_ in range(2):
                P.MM(pskt[:, dc * 128:(dc + 1) * 128], kp[p_][:, dc, :], C.ident_b[:], start=(p_ == 0), stop=(p_ == 1))
        kd = r_kd.get()
        P.TT(P.dve, kd[:], pskt[:, 0:256], ed[:], ALU.mult)
        pso = lin_attn_core(P, C, psum, W, qt, kt, kd, gv[:], el, em, gS32, gSbf, 2, 512)
        rms_out(P, C, W, pso, 512, gng, ggT, out_g, t0, "g")
        if stop_after == "gla":
            continue

        hq = r_hq.get(); hf = r_hf.get(); hvb = r_hv.get()
        P.dma(P.sp, hq[:], hqT[:, :, t0:t0 + 128])
        P.dma(P.sp, hf[:], hfT[:, :, t0:t0 + 128])
        P.dma(P.sp, hvb[:], hv[t0:t0 + 128, 512:1024])
        t1 = r_t1.get(); fT = r_fT.get(); gT = r_gT.get()
        P.ACT(t1[:], hf[:], AF.Exp, scale=-1.0)
        P.TS(P.dve, t1[:], t1[:], 1.0, None, ALU.add)
        P.RECIP(t1[:], t1[:])
        for hd in range(4):
            P.TS(P.dve, fT[:, hd, :], t1[:, hd, :], omlb[:, hd:hd + 1], lb[:, hd:hd + 1], ALU.mult, ALU.add)
        P.ACT(gT[:], fT[:], AF.Ln)
        gbp = [r.get() for r in r_gbp]
        gbt = [r.get() for r in r_gbt]
        split_bf16(P, gT[:], [x[:] for x in gbp], [x[:] for x in gbt])
        gtp = [r.get() for r in r_gtp]
        for p_ in range(3):
            pst_ = C.psb.get()
            for hd in range(4):
                P.TR(pst_[:, hd * 128:(hd + 1) * 128], gbp[p_][:, hd, :], C.ident_b[:])
            P.anycp(gtp[p_][:], pst_[:, 0:512].rearrange("p (h d) -> p h d", h=4))
        heb = r_heb.get(); hei = r_hei.get(); hbp = r_hbp.get(); hbn = r_hbn.get(); hel = r_hel.get()
        hqt = r_hqt.get(); hkt = r_hkt.get(); hed = r_hed.get(); hkd = r_hkd.get(); hem = r_hem.get()
        t2 = r_t1.get()
        silu_gate(P, t2[:], hq[:], r_stmp.get()[:])
        for hd in range(4):
            psb4 = psum.get()
            for p_ in range(3):
                P.MM(psb4[:, 0:128], gtp[p_][:, hd, :], C.U_b[:], start=(p_ == 0), stop=(p_ == 2))
            b_ = psb4[:, 0:128]
            P.TS(P.dve, hbn[:, hd:hd + 1], b_[:, 63:64], -1.0, None, ALU.mult)
            P.CP(P.dve, hbp[:, hd:hd + 1], b_[:, 63:64])
            P.ACT(heb[:, hd, :], b_, AF.Exp, bias=hbn[:, hd:hd + 1], scale=1.0)
            P.ACT(hei[:, hd, :], b_, AF.Exp, bias=hbp[:, hd:hd + 1], scale=-1.0)
            P.ACT(hel[:, hd:hd + 1], b_[:, 127:128], AF.Exp)
            P.ACT(hem[:, hd:hd + 1], b_[:, 63:64], AF.Exp)
            psr4 = psum.get()
            for p_ in range(3):
                P.MM(psr4[:, 0:128], C.SU_b[:], gtp[p_][:, hd, :], start=(p_ == 0), stop=(p_ == 2))
            P.ACT(hed[:, hd, :], psr4[:, 0:128], AF.Exp)
        P.TT(P.dve, hqt[:], t2[:], heb[:], ALU.mult)
        P.TS(P.dve, fT[:], fT[:], -1.0, 1.0, ALU.mult, ALU.add)
        P.TT(P.dve, hkt[:], fT[:], hei[:], ALU.mult)
        op_ = [r.get() for r in r_op]
        otmp = r_stmp.get()
        split_bf16(P, fT[:], [x[:] for x in op_], [otmp[:]])
        for hd in range(4):
            pso_ = psum.get()
            for p_ in range(2):
                P.MM(pso_[:, 0:128], op_[p_][:, hd, :], C.ident_b[:], start=(p_ == 0), stop=(p_ == 1))
            P.TT(P.dve, hkd[:, hd, :], pso_[:, 0:128], hed[:, hd, :], ALU.mult)
        for hd in range(4):
            pso = lin_attn_core(P, C, psum, W, hqt[:, hd:hd + 1, :], hkt[:, hd:hd + 1, :], hkd[:, hd, :],
                                hvb[:, hd * 128:(hd + 1) * 128], hel[:, hd:hd + 1], hem[:, hd:hd + 1], hS32[hd], hSbf[hd], 1, 128)
            hgT_h = hT.view(hTt[2048 + 128 * hd:2176 + 128 * hd, :].rearrange("(c p) t -> p c t", p=128))
            rms_out(P, C, W, pso, 128, hng, hgT_h, out_h[hd], t0, "h")
    P.dma(P.pool, io["gS_out"][:, :, :], gS32[:])
    for hd in range(4):
        P.dma(P.pool, io["hS_out"][hd], hS32[hd][:])


def new_prog():
    nc = bass.Bass("TRN2", target_bir_lowering=False)
    P = Prog(nc)
    C = Consts(P)
    psum = Rot(lambda i: P.ps("psum", [128, 512], F32), 6)
    C.psb = Rot(lambda i: P.ps("psb", [128, 1024], BF16), 2)
    return nc, P, C, psum


def build_l0a(T, stop_after=None, debug=True):
    nc, P, C, psum = new_prog()
    io = {}
    ext = lambda n, s, dt=F32: P.dram(n, s, dt, kind="ExternalInput")
    io["xT"] = ext("xT", [D, T])
    io["w_fm"] = ext("w_fm", [NCH0, 128, 32, 128])
    io["w_tm"] = ext("w_tm", [2, 128, 32, 512])
    io["w_a2"] = ext("w_a2", [16, 256])
    io["b_a"] = ext("b_a", [1, 256])
    io["gla_ng"] = ext("gla_ng", [128, 4])
    io["hgrn_ng"] = ext("hgrn_ng", [128, 1])
    io["lb_logits"] = ext("lb_logits", [128, 3, 4])
    io["memT"] = ext("memT", [D, 256])
    io["w_mk"] = ext("w_mk", [128, 32, 128])
    io["w_mv"] = ext("w_mv", [128, 32, 128])
    kd = "ExternalOutput" if debug else "Internal"
    io["hT"] = P.dram("hT", [NCH0 * 128, T], F32, kind=kd)
    io["hv"] = P.dram("hv", [T, 1024], BF16, kind=kd)
    io["xTb"] = P.dram("xTb", [D, T], BF16)
    io["w_fmb"] = P.dram("w_fmb", [NCH0, 128, 32, 128], BF16)
    io["w_tmb"] = P.dram("w_tmb", [2, 128, 32, 512], BF16)
    io["outT"] = P.dram("outT", [1152, T], BF16, kind="ExternalOutput")
    io["gS_in"] = ext("gS_in", [128, 2, 512])
    io["hS_in"] = ext("hS_in", [4, 128, 1, 128])
    io["gS_out"] = P.dram("gS_out", [128, 2, 512], F32, kind="ExternalOutput")
    io["hS_out"] = P.dram("hS_out", [4, 128, 1, 128], F32, kind="ExternalOutput")
    if stop_after == "h2":
        io["dbg"] = P.dram("dbg", [T // 128, 6, 128, 4, 128], F32, kind="ExternalOutput")
    phase_l0a(P, C, T, io, psum, stop_after)
    P.finish()
    return nc, P


def tile_w_fm(w, ncols_pad):
    K, N = w.shape
    wp = np.zeros((K, ncols_pad), np.float32)
    wp[:, :N] = w
    return np.ascontiguousarray(wp.reshape(K // 128, 128, ncols_pad // 128, 128).transpose(2, 1, 0, 3))


def tile_w_tm(w):
    K, N = w.shape
    return np.ascontiguousarray(w.reshape(K // 128, 128, N // 512, 512).transpose(2, 1, 0, 3))


def l0a_inputs(inp, b, g, T):
    w = inp["ev_w_in"][0]
    o = np.cumsum([0, 1024, 1024, 2048, 2048, 16, 2048, 2048, 2048, 2048, 512, 512])
    gq, gk, gv, gg, ga, hq, hf, hi, hg, mq, mg = [w[:, o[i]:o[i + 1]] for i in range(11)]
    sl = lambda m, width: m[:, g * width:(g + 1) * width]
    fm = np.concatenate([sl(gq, 256), sl(gk, 256), sl(gg, 512), sl(hq, 512), sl(hf, 512), sl(hg, 512), sl(mq, 128),
                         sl(mg, 128), ga], axis=1)
    tm = np.concatenate([sl(gv, 512), sl(hi, 512)], axis=1)
    d = {}
    d["xT"] = np.ascontiguousarray(inp["x"][b, :T].T)
    d["w_fm"] = tile_w_fm(fm, NCH0 * 128)
    d["w_tm"] = tile_w_tm(tm)
    d["w_a2"] = np.ascontiguousarray(inp["ev_gla_w_a2"][0][:, g * 256:(g + 1) * 256])
    d["b_a"] = np.ascontiguousarray(inp["ev_gla_b_a"][0][None, g * 256:(g + 1) * 256])
    d["gla_ng"] = np.ascontiguousarray(inp["ev_gla_norm_g"][0].reshape(4, 128).T)
    d["hgrn_ng"] = np.ascontiguousarray(inp["ev_hgrn_norm_g"][0].reshape(128, 1))
    lbl = inp["hgrn_lb_logits"][:, g * 512:(g + 1) * 512]
    d["lb_logits"] = np.ascontiguousarray(lbl.reshape(3, 4, 128).transpose(2, 0, 1))
    d["memT"] = np.ascontiguousarray(inp["mem"][b].T)
    tk = lambda m: np.ascontiguousarray(m[:, g * 128:(g + 1) * 128].reshape(32, 128, 128).transpose(1, 0, 2))
    d["w_mk"] = tk(inp["ev_mem_w_k"][0])
    d["w_mv"] = tk(inp["ev_mem_w_v"][0])
    return d


def phase_b(P, C, TB, K, io, psum, want_T):
    KC = K // 128
    KG = KC // 4
    NB = 2 if TB % 256 == 0 else 1
    TT = NB * 128
    catT = io["catT"]; xres = io["xres"]; xo = io["xo"]
    cast_dram(P, io["w_out"], io["w_outb"], 8 * 128, KC * 512)
    wl = io["w_outb"]
    gbc = P.sb("gbc", [128, 4096], F32)
    bbc = P.sb("bbc", [128, 4096], F32)
    P.dma(P.sp, gbc[:], io["ln_g"].view(io["ln_g"].t.partition_broadcast(128)))
    P.dma(P.sp, bbc[:], io["ln_b"].view(io["ln_b"].t.partition_broadcast(128)))
    eps5 = P.sb("eps5", [128, 1], F32)
    P.MEMSET(P.dve, eps5[:], 1e-5)
    aTs = Rot(lambda i: P.sb("b_aT", [128, KC, TT], BF16), 1)
    wts = Rot(lambda i: P.sb("b_wt", [128, KG, 512], BF16), 3)
    zs = Rot(lambda i: P.sb("b_z", [128, NB, 4096], F32), 1)
    xrs = Rot(lambda i: P.sb("b_xr", [128, 512], F32), 3)
    sts = Rot(lambda i: P.sb("b_stats", [128, 8, 6], F32), 2)
    mvs = Rot(lambda i: P.sb("b_mv", [128, 2], F32), 2)
    rss = Rot(lambda i: P.sb("b_rs", [128, 2], F32), 2)
    obs = Rot(lambda i: P.sb("b_ob", [128, 4096], BF16), 1)
    oTs = Rot(lambda i: P.sb("b_oT", [128, 8, 128], BF16), 2)
    cv = catT.t.rearrange("(kc p) t -> p kc t", p=128)
    for tt in range(TB // TT):
        aT = aTs.get()
        h2 = KC // 2
        P.dma(P.sp, aT[:, 0:h2, :], catT.view(cv[:, 0:h2, tt * TT:(tt + 1) * TT]), acc=True)
        P.dma(P.sp, aT[:, h2:KC, :], catT.view(cv[:, h2:KC, tt * TT:(tt + 1) * TT]), acc=True)
        z = zs.get()
        for n in range(8):
            pss = [psum.get() for _ in range(NB)]
            for kg in range(4):
                wt = wts.get()
                P.dma(P.sp, wt[:], wl[n, :, kg * KG:(kg + 1) * KG, :])
                for b in range(NB):
                    for k2 in range(KG):
                        kc = kg * KG + k2
                        P.MM(pss[b][:, :], aT[:, kc, b * 128:(b + 1) * 128], wt[:, k2, :], start=(kc == 0), stop=(kc == KC - 1))
            for b in range(NB):
                xr = xrs.get()
                t0 = tt * TT + b * 128
                P.dma(P.sp, xr[:], xres[t0:t0 + 128, n * 512:(n + 1) * 512])
                P.STT(P.dve, z[:, b, n * 512:(n + 1) * 512], xr[:], ALPHA, pss[b][:, :], ALU.mult, ALU.add, acc=True)
        for b in range(NB):
            t0 = tt * TT + b * 128
            st = sts.get(); mv = mvs.get(); rs = rss.get()
            for c in range(8):
                P.op(P.dve, lambda h, c=c: h.bn_stats(out=st[:, c, :].ap, in_=z[:, b, c * 512:(c + 1) * 512].ap), [z[:]], [st[:]], acc=True)
            P.op(P.dve, lambda h: h.bn_aggr(out=mv[:].ap, in_=st[:].ap), [st[:]], [mv[:]])
            P.ACT(rs[:, 0:1], mv[:, 1:2], AF.Ln, bias=eps5[:])
            P.ACT(rs[:, 0:1], rs[:, 0:1], AF.Exp, scale=-0.5)
            P.STT(P.dve, rs[:, 1:2], mv[:, 0:1], -1.0, rs[:, 0:1], ALU.mult, ALU.mult)
            P.TS(P.dve, z[:, b, :], z[:, b, :], rs[:, 0:1], rs[:, 1:2], ALU.mult, ALU.add)
            P.TT(P.pool, z[:, b, :], z[:, b, :], gbc[:], ALU.mult)
            P.TT(P.dve, z[:, b, :], z[:, b, :], bbc[:], ALU.add)
            P.dma(P.sp, xo[t0:t0 + 128, :], z[:, b, :], acc=True)
            if want_T:
                ob = obs.get()
                P.CP(P.act, ob[:], z[:, b, :])
                xoT = io["xoT"]
                xv = xoT.t.rearrange("(c p) t -> p c t", p=128)
                for q4 in range(4):
                    pst = C.psb.get()
                    for c8 in range(8):
                        c = q4 * 8 + c8
                        P.TR(pst[:, c8 * 128:(c8 + 1) * 128], ob[:, c * 128:(c + 1) * 128], C.ident_b[:])
                    oT = oTs.get()
                    P.anycp(oT[:], pst[:, :].rearrange("p (c t) -> p c t", c=8))
                    P.dma(P.sp, xoT.view(xv[:, q4 * 8:(q4 + 1) * 8, t0:t0 + 128]), oT[:], acc=True)


def build_b(TB, K, want_T):
    nc, P, C, psum = new_prog()
    io = {}
    ext = lambda n, s, dt=F32: P.dram(n, s, dt, kind="ExternalInput")
    io["catT"] = ext("catT", [K, TB], BF16)
    io["w_out"] = ext("w_out", [8, 128, K // 128, 512])
    io["w_outb"] = P.dram("w_outb", [8, 128, K // 128, 512], BF16)
    io["xres"] = ext("xres", [TB, 4096])
    io["ln_g"] = ext("ln_g", [1, 4096])
    io["ln_b"] = ext("ln_b", [1, 4096])
    io["xo"] = P.dram("xo", [TB, 4096], F32, kind="ExternalOutput")
    if want_T:
        io["xoT"] = P.dram("xoT", [4096, TB], BF16, kind="ExternalOutput")
    with P.scope():
        phase_b(P, C, TB, K, io, psum, want_T)
    P.finish()
    return nc, P


NFM1 = 4352
NCH1 = 34


def phase_l1a1(P, C, T, io, psum):
    h1T = io["h1T"]
    cast_dram(P, io["w_fm"], io["w_fmb"], NCH1 * 128, 4096)
    with P.scope():
        gemm_fm(P, io["xT"], D, T, io["w_fmb"], NCH1, h1T, NFM1, psum)
    TT = min(256, T)
    NB = TT // 128
    cw = P.sb("cw", [128, 16, 4], F32)
    cb = P.sb("cb", [128, 16], F32)
    sk = P.sb("sk", [128, 16], F32)
    P.dma(P.sp, cw[:], io["conv_w"][:, :, :])
    P.dma(P.sp, cb[:], io["conv_b"][:, :])
    P.dma(P.sp, sk[:], io["skip"][:, :])
    bd = {}
    for nm in ("bdq", "bdk", "bdv"):
        bd[nm] = P.sb(nm, [128, 16, 128], BF16)
        P.dma(P.pool, bd[nm][:], io[nm][:, :, :])
    wif = P.sb("wif", [128, 3, 16, 8], BF16)
    P.dma(P.pool, wif[:], io["w_if"][:, :, :, :])
    xms = Rot(lambda i: P.sb("xm", [128, 16, TT + 3], F32), 1)
    accs = Rot(lambda i: P.sb("acc", [128, 16, TT], F32), 1)
    tmps = Rot(lambda i: P.sb("ctmp", [128, 16, TT], F32), 1)
    xcbs = Rot(lambda i: P.sb("xcb", [128, 16, TT], BF16), 1)
    xmbs = Rot(lambda i: P.sb("xmb", [128, 16, TT], BF16), 1)
    sxcs = Rot(lambda i: P.sb("sxc", [128, 16, TT], BF16), 1)
    fmo = {nm: Rot(lambda i, nm=nm: P.sb("o_" + nm, [128, 16, TT], BF16), 1) for nm in ("q", "k", "v")}
    tmo = {nm: Rot(lambda i, nm=nm: P.sb("t_" + nm, [128, 2048], BF16), 2) for nm in ("k", "v")}
    gps = Rot(lambda i: P.sb("gp", [128, 8], F32), 2)
    xv = h1T.t[0:2048, :].rearrange("(c p) t -> p c t", p=128)
    dT = {nm: io[nm + "T"] for nm in ("q", "k")}
    dTv = {nm: dT[nm].t.rearrange("(c p) t -> p c t", p=128) for nm in dT}
    sxv = io["sxcT"].t.rearrange("(c p) t -> p c t", p=128)
    for tt in range(T // TT):
        t0 = tt * TT
        xm = xms.get()
        if tt == 0:
            P.dma(P.pool, xm[:, :, 0:3], io["halo_in"][:, :, :], acc=True)
            P.dma(P.sp, xm[:, 0:8, 3:TT + 3], h1T.view(xv[:, 0:8, 0:TT]), acc=True)
            P.dma(P.sp, xm[:, 8:16, 3:TT + 3], h1T.view(xv[:, 8:16, 0:TT]), acc=True)
        else:
            P.dma(P.sp, xm[:, 0:8, :], h1T.view(xv[:, 0:8, t0 - 3:t0 + TT]), acc=True)
            P.dma(P.sp, xm[:, 8:16, :], h1T.view(xv[:, 8:16, t0 - 3:t0 + TT]), acc=True)
        acc = accs.get()
        for c in range(16):
            P.TS(P.dve, acc[:, c, :], xm[:, c, 0:TT], cw[:, c, 0:1], cb[:, c:c + 1], ALU.mult, ALU.add, acc=True)
            for j in (1, 2, 3):
                P.STT(P.dve, acc[:, c, :], xm[:, c, j:j + TT], cw[:, c, j:j + 1], acc[:, c, :], ALU.mult, ALU.add, acc=True)
        xcb = xcbs.get(); xmb = xmbs.get(); sxc = sxcs.get(); tmp = tmps.get()
        silu_gate(P, acc[:], acc[:], tmp[:])
        P.CP(P.act, xcb[:], acc[:])
        P.CP(P.act, xmb[:], xm[:, :, 3:TT + 3])
        for c in range(16):
            P.TS(P.pool, sxc[:, c, :], acc[:, c, :], sk[:, c:c + 1], None, ALU.mult, acc=True)
        P.dma(P.sp, io["sxcT"].view(sxv[:, :, t0:t0 + TT]), sxc[:])
        outs = {}
        for nm, src, w in (("q", xcb, bd["bdq"]), ("k", xcb, bd["bdk"]), ("v", xmb, bd["bdv"])):
            o = fmo[nm].get()
            outs[nm] = o
            for c in range(16):
                ps = psum.get()
                P.MM(ps[:, 0:TT], w[:, c, :], src[:, c, :])
                P.anycp(o[:, c, :], ps[:, 0:TT], acc=True)
            if nm in dT:
                P.dma(P.sp, dT[nm].view(dTv[nm][:, :, t0:t0 + TT]), o[:])
        for b in range(NB):
            bs = slice(b * 128, (b + 1) * 128)
            for nm, src, w in (("k", xcb, bd["bdk"]), ("v", xmb, bd["bdv"])):
                o = tmo[nm].get()
                for c4 in range(4):
                    ps = psum.get()
                    for c in range(c4 * 4, c4 * 4 + 4):
                        P.MM(ps[:, (c % 4) * 128:(c % 4 + 1) * 128], src[:, c, bs], w[:, c, :])
                    P.anycp(o[:, c4 * 512:(c4 + 1) * 512], ps[:, :], acc=True)
                P.dma(P.sp, io[nm + "_tm"][t0 + b * 128:t0 + (b + 1) * 128, :], o[:], acc=True)
            ps = psum.get()
            n = 0
            for gi, nm in enumerate(("q", "k", "v")):
                for c in range(16):
                    P.MM(ps[:, 0:8], outs[nm][:, c, bs], wif[:, gi, c, :], start=(n == 0), stop=(n == 47))
                    n += 1
            gp = gps.get()
            P.CP(P.dve, gp[:], ps[:, 0:8])
            P.dma(P.sp, io["gp"][t0 + b * 128:t0 + (b + 1) * 128, :], gp[:], acc=True)
        if tt == T // TT - 1:
            P.dma(P.pool, io["halo_out"][:, :, :], xm[:, :, TT:TT + 3])


def build_l1a1(T):
    nc, P, C, psum = new_prog()
    io = {}
    ext = lambda n, s, dt=F32: P.dram(n, s, dt, kind="ExternalInput")
    out = lambda n, s, dt=F32: P.dram(n, s, dt, kind="ExternalOutput")
    io["xT"] = ext("xT", [D, T], BF16)
    io["w_fm"] = ext("w_fm", [NCH1, 128, 32, 128])
    io["w_fmb"] = P.dram("w_fmb", [NCH1, 128, 32, 128], BF16)
    io["conv_w"] = ext("conv_w", [128, 16, 4])
    io["conv_b"] = ext("conv_b", [128, 16])
    io["skip"] = ext("skip", [128, 16])
    for nm in ("bdq", "bdk", "bdv"):
        io[nm] = ext(nm, [128, 16, 128])
    io["w_if"] = ext("w_if", [128, 3, 16, 8])
    io["h1T"] = out("h1T", [NCH1 * 128, T])
    io["qT"] = out("qT", [2048, T], BF16)
    io["kT"] = out("kT", [2048, T], BF16)
    io["sxcT"] = out("sxcT", [2048, T], BF16)
    io["k_tm"] = out("k_tm", [T, 2048], BF16)
    io["v_tm"] = out("v_tm", [T, 2048], BF16)
    io["gp"] = out("gp", [T, 8])
    io["halo_in"] = ext("halo_in", [128, 16, 3])
    io["halo_out"] = out("halo_out", [128, 16, 3])
    phase_l1a1(P, C, T, io, psum)
    P.finish()
    return nc, P


def blockdiag(w):
    out = np.zeros((16, 128, 128), np.float32)
    wb = w.reshape(16, 32, 4, 4)
    for n in range(32):
        out[:, 4 * n:4 * n + 4, 4 * n:4 * n + 4] = wb[:, n]
    return np.ascontiguousarray(out.transpose(1, 0, 2))


def l1a1_inputs(inp, x1T_b, h, T):
    w = inp["od_w_in"][0]
    hs = slice(h * 2048, (h + 1) * 2048)
    fm = np.concatenate([w[:, 0:8192][:, hs], w[:, 8192:16384][:, hs], w[:, 16384 + h * 128:16384 + (h + 1) * 128],
                         w[:, 16896 + h * 128:16896 + (h + 1) * 128]], axis=1)
    d = {"xT": x1T_b, "w_fm": tile_w_fm(fm, NCH1 * 128)}
    pc = lambda v: np.ascontiguousarray(v[hs].reshape(16, 128).T)
    d["conv_w"] = np.ascontiguousarray(inp["od_conv_w"][0][:, hs].reshape(4, 16, 128).transpose(2, 1, 0))
    d["conv_b"] = pc(inp["od_conv_b"][0])
    d["skip"] = pc(inp["od_skip"][0])
    bs = slice(h * 512, (h + 1) * 512)
    d["bdq"] = blockdiag(inp["od_w_q"][0][bs]); d["bdk"] = blockdiag(inp["od_w_k"][0][bs]); d["bdv"] = blockdiag(inp["od_w_v"][0][bs])
    wif = inp["od_w_if"][0].reshape(3, 8192, 8)[:, hs]
    d["w_if"] = np.ascontiguousarray(wif.reshape(3, 16, 128, 8).transpose(2, 0, 1, 3))
    return d


def phase_l1a2(P, C, T, io, psum):
    h1T = io["h1T"]
    hTt = h1T.t
    KT = P.sb("memKT2", [128, 256], BF16)
    V = P.sb("memV2", [128, 2, 128], BF16)
    with P.scope():
        KT_, V_ = mem_prep(P, C, io["memT"], io["w_mk"], io["w_mv"], psum)
        P.CP(P.dve, KT[:], KT_[:])
        P.CP(P.dve, V[:], V_[:])
    W = {"psb": C.psb}
    rot = lambda name, shape, dt, n=2: Rot(lambda i: P.sb(name, shape, dt), n)
    for nm, shape, dt in [("ma_q32", [128, 128], F32), ("ma_g32", [128, 128], F32), ("ma_qb", [128, 128], BF16),
                          ("ma_v1", [128, 1], F32), ("ma_v2", [128, 1], F32), ("ma_e", [128, 256], F32),
                          ("ma_pb", [128, 256], BF16), ("ma_pT", [128, 2, 128], BF16), ("ma_gate", [128, 128], F32),
                          ("ma_tmp", [128, 128], F32), ("ma_ob", [128, 128], BF16)]:
        W[nm] = rot(nm, shape, dt)
    mqT = h1T[4096:4224, :]
    mgT = h1T[4224:4352, :]
    zTv = h1T.view(hTt[2048:4096, :].rearrange("(c p) t -> p c t", p=128))
    outT = io["outT"]
    out_x = outT.view(outT.t[0:2048, :].rearrange("(c p) t -> p c t", p=128))
    out_m = outT[2048:2176, :]
    fmv = lambda nm: io[nm].view(io[nm].t.rearrange("(c p) t -> p c t", p=128))
    qTv, kTv, sxv = fmv("qT"), fmv("kT"), fmv("sxcT")

    ng = P.sb("mhng", [128, 16], F32)
    P.dma(P.sp, ng[:], io["mh_ng"][:, :])
    b2 = P.sb("b2", [128, 2], F32)
    P.dma(P.sp, b2[:], io["b_if2"].view(io["b_if2"].t.partition_broadcast(128)))
    eps6 = P.sb("eps6b", [128, 1], F32)
    P.MEMSET(P.dve, eps6[:], 1e-6)
    EC = P.sb("EC", [128, 16, 16], BF16)
    P.MEMSET(P.pool, EC[:], 0.0)
    for c in range(16):
        P.MEMSET(P.pool, EC[:, c, c:c + 1], 1.0)
    Cb = P.sb("Cb", [128, 16, 2048], BF16)
    for c in range(0, 16, 4):
        P.dma(P.pool, Cb[:, c:c + 4, :], io["C_in"][:, c:c + 4, :], acc=True)
    nv32 = P.sb("nv32", [128, 16], F32)
    nvb = P.sb("nvb", [128, 16], BF16)
    mrun = P.sb("mrun", [128, 1], F32)
    P.dma(P.pool, nv32[:], io["nv_in"][:, :])
    P.dma(P.pool, mrun[:], io["m_in"][:, :])
    P.CP(P.dve, nvb[:], nv32[:])
    SC = 2048 ** -0.5

    r_q = rot("m_q", [128, 16, 128], BF16); r_k = rot("m_k", [128, 16, 128], BF16)
    r_sx = rot("m_sx", [128, 16, 128], BF16); r_z = rot("m_z", [128, 16, 128], F32)
    r_vt = rot("m_vt", [128, 2048], BF16); r_kt = rot("m_kt", [128, 2048], BF16)
    r_gp = rot("m_gp", [128, 4, 2], F32)
    r_g2 = rot("m_g2", [128, 2], F32)
    sm = {nm: rot("m_" + nm, [128, 1], F32) for nm in ("lf", "inter", "rmax", "mi", "nmi", "winter", "elim", "lw", "nmn", "wj",
                                                        "carry", "tcar", "rsum", "den", "rden", "s2", "bcol", "ccol")}
    r_R1 = rot("m_R1", [128, 128], F32, 1)
    r_lf3 = rot("m_lf3", [128, 3], BF16); r_c1t = rot("m_c1t", [128, 2], F32, 4); r_cb = rot("m_cb", [128, 1], BF16, 4)
    r_dg = [rot("m_dg%d" % i, [128, 128], BF16, 1) for i in range(3)]
    r_v6 = rot("m_v6", [128, 3, 2], BF16); r_v2t = rot("m_v2t", [128, 2, 2], F32); r_bc6 = rot("m_bc6", [128, 3, 2], F32)
    r_wi = rot("m_wi", [128, 128], F32, 1)
    r_v2 = rot("m_v2", [128, 2], F32); r_bc2 = rot("m_bc2", [128, 2], F32)
    r_sc = rot("m_sc", [128, 128], BF16); r_scT = rot("m_scT", [128, 128], BF16)
    r_h = rot("m_h", [128, 2048], F32, 1); r_t5 = rot("m_t5", [128, 512], F32, 2)
    r_hn = rot("m_hn", [128, 2048], BF16, 1)
    r_st = rot("m_st", [128, 4, 6], F32); r_mv = rot("m_mv", [128, 2], F32); r_rs = rot("m_rs", [128, 2], F32)
    r_gate = rot("m_gate", [128, 16, 128], F32, 1); r_gt = rot("m_gt", [128, 16, 128], F32, 1)
    r_o1 = rot("m_o1", [128, 16, 128], F32, 1); r_ob = rot("m_ob", [128, 16, 128], BF16, 2)
    r_kw = rot("m_kw", [128, 2048], BF16, 1)

    for ck in range(T // 128):
        t0 = ck * 128
        ts = slice(t0, t0 + 128)
        mem_attn_block(P, C, KT, V, mqT, mgT, out_m, t0, psum, W)
        q = r_q.get(); k = r_k.get(); sx = r_sx.get(); z = r_z.get(); vt = r_vt.get(); kt = r_kt.get(); gp = r_gp.get()
        P.dma(P.sp, q[:], qTv[:, :, ts]); P.dma(P.sp, k[:], kTv[:, :, ts]); P.dma(P.sp, sx[:], sxv[:, :, ts])
        P.dma(P.sp, z[:], zTv[:, :, ts])
        P.dma(P.sp, vt[:], io["v_tm"][ts, :]); P.dma(P.sp, kt[:], io["k_tm"][ts, :])
        P.dma(P.sp, gp[:], io["gp4"].view(io["gp4"].t[:, ts, :].rearrange("g t c -> t g c")))
        g2 = r_g2.get()
        P.TT(P.dve, g2[:], gp[:, 0, :], gp[:, 1, :], ALU.add)
        P.TT(P.dve, g2[:], g2[:], gp[:, 2, :], ALU.add)
        P.TT(P.dve, g2[:], g2[:], gp[:, 3, :], ALU.add)
        P.TT(P.dve, g2[:], g2[:], b2[:], ALU.add)
        ic = g2[:, 0:1]
        lf = sm["lf"].get()
        P.ACT(lf[:], g2[:, 1:2], AF.Exp, scale=-1.0)
        P.ACT(lf[:], lf[:], AF.Ln, bias=1.0)
        P.TS(P.dve, lf[:], lf[:], -1.0, None, ALU.mult)
        lf3 = r_lf3.get(); lft = r_c1t.get()
        split_bf16(P, lf[:], [lf3[:, i:i + 1] for i in range(3)], [lft[:, 0:1], lft[:, 1:2]])
        psV = psum.get()
        P.MM(psV[:, 0:3], C.U_b[:], lf3[:])
        bcol = sm["bcol"].get()
        P.RED(P.dve, bcol[:], psV[:, 0:3], ALU.add)
        ccol = sm["ccol"].get(); cb1 = r_cb.get(); cb2 = r_cb.get(); cr = r_c1t.get()
        P.TT(P.dve, ccol[:], ic, bcol[:], ALU.subtract)
        dg = [r.get() for r in r_dg]
        P.TS(P.dve, dg[0][:], C.ident_f[:], ccol[:], None, ALU.mult)
        P.CP(P.act, cb1[:], ccol[:])
        P.TT(P.dve, cr[:, 0:1], ccol[:], cb1[:], ALU.subtract)
        P.TS(P.dve, dg[1][:], C.ident_f[:], cr[:, 0:1], None, ALU.mult)
        P.CP(P.act, cb2[:], cr[:, 0:1])
        P.TT(P.dve, cr[:, 1:2], cr[:, 0:1], cb2[:], ALU.subtract)
        P.TS(P.dve, dg[2][:], C.ident_f[:], cr[:, 1:2], None, ALU.mult)
        psD = psum.get()
        for p_ in range(3):
            P.MM(psD[:, 0:128], C.ONES_b[:], dg[p_][:], start=(p_ == 0), stop=(p_ == 2))
        Dsb = r_R1.get()
        P.STT(P.dve, Dsb[:], psD[:, 0:128], bcol[:], C.NEGM[:], ALU.add, ALU.add)
        inter = sm["inter"].get(); rmax = sm["rmax"].get(); mi = sm["mi"].get(); nmi = sm["nmi"].get()
        P.TT(P.dve, inter[:], bcol[:], mrun[:], ALU.add)
        P.RED(P.dve, rmax[:], Dsb[:], ALU.max)
        P.TT(P.dve, mi[:], rmax[:], inter[:], ALU.max)
        P.TS(P.dve, nmi[:], mi[:], -1.0, None, ALU.mult)
        wi = r_wi.get(); winter = sm["winter"].get(); elim = sm["elim"].get()
        P.ACT(wi[:], Dsb[:], AF.Exp, bias=nmi[:])
        P.ACT(winter[:], inter[:], AF.Exp, bias=nmi[:])
        P.ACT(elim[:], nmi[:], AF.Exp)
        v2 = r_v2.get(); bc2 = r_bc2.get(); v6 = r_v6.get(); v2t = r_v2t.get(); bc6 = r_bc6.get()
        P.CP(P.dve, v2[:, 0:1], mi[:]); P.CP(P.dve, v2[:, 1:2], bcol[:])
        split_bf16(P, v2[:], [v6[:, i, :] for i in range(3)], [v2t[:, 0, :], v2t[:, 1, :]])
        psB = psum.get()
        P.MM(psB[:, 0:6], C.ELAST_b[:], v6[:].rearrange("p a b -> p (a b)"))
        P.CP(P.dve, bc6[:], psB[:, 0:6].rearrange("p (a b) -> p a b", a=3))
        P.TT(P.dve, bc2[:], bc6[:, 0, :], bc6[:, 1, :], ALU.add)
        P.TT(P.dve, bc2[:], bc2[:], bc6[:, 2, :], ALU.add)
        lw = sm["lw"].get(); nmn = sm["nmn"].get(); wj = sm["wj"].get(); carry = sm["carry"].get(); tcar = sm["tcar"].get()
        P.STT(P.dve, lw[:], bcol[:], -1.0, bc2[:, 1:2], ALU.mult, ALU.add)
        P.TT(P.dve, lw[:], lw[:], ic, ALU.add)
        P.TS(P.dve, nmn[:], bc2[:, 0:1], -1.0, None, ALU.mult)
        P.ACT(wj[:], lw[:], AF.Exp, bias=nmn[:])
        P.TS(P.dve, wj[:], wj[:], SC, None, ALU.mult)
        P.TT(P.dve, tcar[:], bc2[:, 1:2], mrun[:], ALU.add)
        P.ACT(carry[:], tcar[:], AF.Exp, bias=nmn[:])
        P.CP(P.dve, mrun[:], bc2[:, 0:1])
        psS = psum.get()
        for c in range(16):
            P.MM(psS[:, 0:128], q[:, c, :], k[:, c, :], start=(c == 0), stop=(c == 15))
        sc = r_sc.get(); rsum = sm["rsum"].get()
        P.STT(P.dve, sc[:], psS[:, 0:128], SC, wi[:], ALU.mult, ALU.mult, accum=rsum[:])
        pst = C.psb.get()
        P.TR(pst[:, 0:128], sc[:], C.ident_b[:])
        scT = r_scT.get()
        P.CP(P.act, scT[:], pst[:, 0:128])
        psQ = psum.get()
        for c in range(16):
            P.MM(psQ[:, 0:1], q[:, c, :], nvb[:, c:c + 1], start=(c == 0), stop=(c == 15))
        den = sm["den"].get(); rden = sm["rden"].get(); s2 = sm["s2"].get()
        P.STT(P.dve, den[:], psQ[:, 0:1], winter[:], rsum[:], ALU.mult, ALU.add)
        nden = sm["tcar"].get()
        P.TS(P.dve, nden[:], den[:], -1.0, None, ALU.mult)
        P.TT(P.dve, den[:], den[:], nden[:], ALU.max)
        P.TT(P.dve, den[:], den[:], elim[:], ALU.max)
        P.RECIP(rden[:], den[:])
        P.TT(P.dve, s2[:], winter[:], rden[:], ALU.mult)
        h32 = r_h.get()
        for s in range(4):
            vs = slice(s * 512, (s + 1) * 512)
            p1 = psum.get(); p2 = psum.get()
            P.MM(p1[:, :], scT[:], vt[:, vs])
            for c in range(16):
                P.MM(p2[:, :], q[:, c, :], Cb[:, c, vs], start=(c == 0), stop=(c == 15))
            t5 = r_t5.get()
            P.ACT(t5[:], p2[:, :], AF.Copy, scale=s2[:])
            P.STT(P.dve, h32[:, vs], p1[:, :], rden[:], t5[:], ALU.mult, ALU.add, acc=True)
        st = r_st.get(); mv = r_mv.get(); rs = r_rs.get()
        for c in range(4):
            P.op(P.dve, lambda hh, c=c: hh.bn_stats(out=st[:, c, :].ap, in_=h32[:, c * 512:(c + 1) * 512].ap), [h32[:]], [st[:]], acc=True)
        P.op(P.dve, lambda hh: hh.bn_aggr(out=mv[:].ap, in_=st[:].ap), [st[:]], [mv[:]])
        P.ACT(rs[:, 0:1], mv[:, 1:2], AF.Ln, bias=eps6[:])
        P.ACT(rs[:, 0:1], rs[:, 0:1], AF.Exp, scale=-0.5)
        P.STT(P.dve, rs[:, 1:2], mv[:, 0:1], -1.0, rs[:, 0:1], ALU.mult, ALU.mult)
        hn = r_hn.get()
        P.TS(P.dve, hn[:], h32[:], rs[:, 0:1], rs[:, 1:2], ALU.mult, ALU.add)
        gate = r_gate.get()
        silu_gate(P, gate[:], z[:], r_gt.get()[:])
        o1 = r_o1.get(); ob = r_ob.get()
        for c8 in range(2):
            pst = C.psb.get()
            for c in range(8):
                cc = c8 * 8 + c
                P.TR(pst[:, c * 128:(c + 1) * 128], hn[:, cc * 128:(cc + 1) * 128], C.ident_b[:])
            for c in range(8):
                cc = c8 * 8 + c
                P.STT(P.dve, o1[:, cc, :], pst[:, c * 128:(c + 1) * 128], ng[:, cc:cc + 1], sx[:, cc, :], ALU.mult, ALU.add, acc=True)
        P.TT(P.pool, ob[:], o1[:], gate[:], ALU.mult)
        P.dma(P.sp, out_x[:, :, ts], ob[:])
        kw = r_kw.get()
        P.TS(P.dve, kw[:], kt[:], wj[:], None, ALU.mult)
        for c in range(16):
            for s in range(4):
                vs = slice(s * 512, (s + 1) * 512)
                pu = psum.get()
                P.MM(pu[:, :], kw[:, c * 128:(c + 1) * 128], vt[:, vs])
                P.STT(P.dve, Cb[:, c, vs], Cb[:, c, vs], carry[:], pu[:, :], ALU.mult, ALU.add, acc=True)
        pn = psum.get()
        for c in range(16):
            P.MM(pn[:, 0:16], kw[:, c * 128:(c + 1) * 128], EC[:, c, :], start=(c == 0), stop=(c == 15))
        P.STT(P.dve, nv32[:], nv32[:], carry[:], pn[:, 0:16], ALU.mult, ALU.add)
        P.CP(P.dve, nvb[:], nv32[:])
    for c in range(0, 16, 4):
        P.dma(P.pool, io["C_out"][:, c:c + 4, :], Cb[:, c:c + 4, :], acc=True)
    P.dma(P.pool, io["nv_out"][:, :], nv32[:])
    P.dma(P.pool, io["m_out"][:, :], mrun[:])


def build_l1a2(T):
    nc, P, C, psum = new_prog()
    io = {}
    ext = lambda n, s, dt=F32: P.dram(n, s, dt, kind="ExternalInput")
    io["h1T"] = ext("h1T", [NCH1 * 128, T])
    io["qT"] = ext("qT", [2048, T], BF16)
    io["kT"] = ext("kT", [2048, T], BF16)
    io["sxcT"] = ext("sxcT", [2048, T], BF16)
    io["k_tm"] = ext("k_tm", [T, 2048], BF16)
    io["v_tm"] = ext("v_tm", [T, 2048], BF16)
    io["gp4"] = ext("gp4", [4, T, 2])
    io["b_if2"] = ext("b_if2", [1, 2])
    io["mh_ng"] = ext("mh_ng", [128, 16])
    io["memT"] = ext("memT", [D, 256])
    io["w_mk"] = ext("w_mk", [128, 32, 128])
    io["w_mv"] = ext("w_mv", [128, 32, 128])
    io["outT"] = P.dram("outT", [2176, T], BF16, kind="ExternalOutput")
    io["C_in"] = ext("C_in", [128, 16, 2048], BF16)
    io["nv_in"] = ext("nv_in", [128, 16])
    io["m_in"] = ext("m_in", [128, 1])
    io["C_out"] = P.dram("C_out", [128, 16, 2048], BF16, kind="ExternalOutput")
    io["nv_out"] = P.dram("nv_out", [128, 16], F32, kind="ExternalOutput")
    io["m_out"] = P.dram("m_out", [128, 1], F32, kind="ExternalOutput")
    phase_l1a2(P, C, T, io, psum)
    P.finish()
    return nc, P


def l1a2_inputs(inp, a1res, b, h):
    r = a1res[h]
    d = {k: r[k] for k in ("h1T", "qT", "kT", "sxcT", "k_tm", "v_tm")}
    d["gp4"] = np.ascontiguousarray(np.stack([a1res[g]["gp"][:, [h, 4 + h]] for g in range(4)], axis=0))
    d["b_if2"] = np.ascontiguousarray(inp["od_b_if"][0][[h, 4 + h]][None, :])
    d["mh_ng"] = np.ascontiguousarray(inp["od_mh_norm_g"][0][h * 2048:(h + 1) * 2048].reshape(16, 128).T)
    d["memT"] = np.ascontiguousarray(inp["mem"][b].T)
    tk = lambda m: np.ascontiguousarray(m[:, h * 128:(h + 1) * 128].reshape(32, 128, 128).transpose(1, 0, 2))
    d["w_mk"] = tk(inp["od_mem_w_k"][0])
    d["w_mv"] = tk(inp["od_mem_w_v"][0])
    return d


def _run(nc, maps):
    return run_bass_kernel_spmd(nc, maps, core_ids=list(range(NCORES))).results


def kernel(**inp):
    inp = {k: np.asarray(v) for k, v in inp.items()}
    B, T, _ = inp["x"].shape
    TS = min(T, SEG_TOKENS)
    NS = T // TS
    TB = TS // 4
    cores = [(b, g) for b in range(2) for g in range(4)]
    bf = ml_dtypes.bfloat16
    w0 = inp["ev_w_out"][0]
    perm0 = np.concatenate([np.r_[g * 512:(g + 1) * 512, 2048 + g * 512:2048 + (g + 1) * 512,
                                  4096 + g * 128:4096 + (g + 1) * 128] for g in range(4)])
    wl0 = tile_w_tm(w0[perm0])
    w1 = inp["od_w_out"][0]
    perm1 = np.concatenate([np.r_[h * 2048:(h + 1) * 2048, 8192 + h * 128:8192 + (h + 1) * 128] for h in range(4)])
    wl1 = tile_w_tm(w1[perm1])
    nc0a, _ = build_l0a(TS, None, debug=False)
    ncb0, _ = build_b(TB, 4608, True)
    nc1a1, _ = build_l1a1(TS)
    nc1a2, _ = build_l1a2(TS)
    ncb1, _ = build_b(TB, 8704, False)
    base0 = [l0a_inputs(inp, b, g, TS) for b, g in cores]
    st0 = [{"gS_in": np.zeros((128, 2, 512), np.float32), "hS_in": np.zeros((4, 128, 1, 128), np.float32)} for _ in cores]
    halo = [np.zeros((128, 16, 3), np.float32) for _ in cores]
    st1 = [{"C_in": np.zeros((128, 16, 2048), bf), "nv_in": np.zeros((128, 16), np.float32),
            "m_in": np.zeros((128, 1), np.float32)} for _ in cores]
    out = np.empty((B, T, D), np.float32)
    for s_ in range(NS):
        seg = slice(s_ * TS, (s_ + 1) * TS)
        maps = []
        for ci, (b, g) in enumerate(cores):
            m = dict(base0[ci])
            m["xT"] = np.ascontiguousarray(inp["x"][b, seg].T)
            m.update(st0[ci])
            maps.append(m)
        r0 = _run(nc0a, maps)
        out0 = [np.asarray(r["outT"]) for r in r0]
        st0 = [{"gS_in": np.asarray(r["gS_out"]), "hS_in": np.asarray(r["hS_out"])} for r in r0]
        del r0
        maps = []
        for b, g in cores:
            sl = slice(g * TB, (g + 1) * TB)
            maps.append({"catT": np.ascontiguousarray(np.concatenate([out0[b * 4 + gg][:, sl] for gg in range(4)], axis=0)),
                         "w_out": wl0, "xres": np.ascontiguousarray(inp["x"][b, seg][sl]),
                         "ln_g": inp["ev_ln_g"], "ln_b": inp["ev_ln_b"]})
        rb0 = _run(ncb0, maps)
        x1 = [np.asarray(r["xo"]) for r in rb0]
        x1T = [np.ascontiguousarray(np.concatenate([np.asarray(rb0[b * 4 + g]["xoT"]) for g in range(4)], axis=1)) for b in range(2)]
        del rb0, out0
        maps = []
        for ci, (b, h) in enumerate(cores):
            m = l1a1_inputs(inp, x1T[b], h, TS)
            m["halo_in"] = halo[ci]
            maps.append(m)
        r1 = _run(nc1a1, maps)
        halo = [np.asarray(r["halo_out"]) for r in r1]
        maps = []
        for ci, (b, h) in enumerate(cores):
            m = l1a2_inputs(inp, r1[b * 4:(b + 1) * 4], b, h)
            m.update(st1[ci])
            maps.append(m)
        r2 = _run(nc1a2, maps)
        out1 = [np.asarray(r["outT"]) for r in r2]
        st1 = [{"C_in": np.asarray(r["C_out"]), "nv_in": np.asarray(r["nv_out"]), "m_in": np.asarray(r["m_out"])} for r in r2]
        del r1, r2
        maps = []
        for b, g in cores:
            sl = slice(g * TB, (g + 1) * TB)
            maps.append({"catT": np.ascontiguousarray(np.concatenate([out1[b * 4 + hh][:, sl] for hh in range(4)], axis=0)),
                         "w_out": wl1, "xres": x1[b * 4 + g], "ln_g": inp["od_ln_g"], "ln_b": inp["od_ln_b"]})
        rb1 = _run(ncb1, maps)
        for ci, (b, g) in enumerate(cores):
            out[b, s_ * TS + g * TB:s_ * TS + (g + 1) * TB] = np.asarray(rb1[ci]["xo"])
        del rb1, out1
    return out
```

```python
import contextlib
import numpy as np
import ml_dtypes
import concourse.bass as bass
import concourse.mybir as mybir
from concourse.bass_utils import run_bass_kernel_spmd

F32 = mybir.dt.float32
BF16 = mybir.dt.bfloat16
AF = mybir.ActivationFunctionType
ALU = mybir.AluOpType
AX = mybir.AxisListType

D = 4096
NCORES = 8
SEG_TOKENS = 2048
NSEM_SP = 16
ALPHA = 4 ** 0.25
GLA_TAU = 16.0


class View:
    __slots__ = ("ap", "buf")

    def __init__(self, ap, buf):
        self.ap = ap
        self.buf = buf

    def __getitem__(self, idx):
        return View(self.ap[idx], self.buf)

    def rearrange(self, pat, **kw):
        return View(self.ap.rearrange(pat, **kw), self.buf)


class Buf:
    __slots__ = ("t", "w", "r", "name")

    def __init__(self, t, name=""):
        self.t = t
        self.w = {}
        self.r = {}
        self.name = name

    def __getitem__(self, idx):
        return View(self.t[idx], self)

    def view(self, ap):
        return View(ap, self)


class Eng:
    def __init__(self, prog, name, h, nsem_dma=0):
        self.name = name
        self.h = h
        self.sem = prog.nc.alloc_semaphore("s_" + name)
        self.cnt = 0
        self.waited = {}
        self.dsems = [prog.nc.alloc_semaphore("d_%s%d" % (name, i)) for i in range(nsem_dma)]
        self.dcnt = [0] * nsem_dma
        self.dnext = 0


def _ap(x):
    return x.ap if isinstance(x, View) else x


class Prog:
    def __init__(self, nc):
        self.nc = nc
        self.pe = Eng(self, "pe", nc.tensor)
        self.act = Eng(self, "act", nc.scalar)
        self.dve = Eng(self, "dve", nc.vector)
        self.pool = Eng(self, "pool", nc.gpsimd, nsem_dma=8)
        self.sp = Eng(self, "sp", nc.sync, nsem_dma=NSEM_SP)
        self.n_ins = 0
        self._uid = 0
        self._flip = 0
        self.stacks = [contextlib.ExitStack()]
        self.all_sems = [e.sem for e in (self.pe, self.act, self.dve, self.pool, self.sp)] + self.pool.dsems + self.sp.dsems
        for sm_ in self.all_sems:
            nc.gpsimd.sem_clear(sm_)
        nc.all_engine_barrier()

    def _wait(self, e, deps):
        for sid, (sem, cnt) in deps.items():
            if e is self.pe and sem is self.pe.sem:
                continue
            if e.waited.get(sid, 0) < cnt:
                e.h.wait_ge(sem, cnt)
                e.waited[sid] = cnt

    @staticmethod
    def _merge(dst, src):
        for sid, (sem, cnt) in src.items():
            if sid not in dst or dst[sid][1] < cnt:
                dst[sid] = (sem, cnt)

    def _deps(self, reads, writes, acc=False):
        deps = {}
        for b in reads:
            self._merge(deps, b.w)
        for b in writes:
            if not acc:
                self._merge(deps, b.w)
            self._merge(deps, b.r)
        return deps

    def _commit(self, st, reads, writes, acc):
        for b in writes:
            if acc:
                self._merge(b.w, st)
            else:
                b.w = dict(st)
                b.r = {}
        for b in reads:
            self._merge(b.r, st)
        self.n_ins += 1

    def op(self, e, fn, reads=(), writes=(), acc=False):
        reads = [v.buf for v in reads if isinstance(v, View)]
        writes = [v.buf for v in writes if isinstance(v, View)]
        self._wait(e, self._deps(reads, writes, acc))
        ins = fn(e.h)
        e.cnt += 1
        ins.then_inc(e.sem, 1)
        self._commit({id(e.sem): (e.sem, e.cnt)}, reads, writes, acc)
        return ins

    def dma(self, e, out, in_, acc=False, **kw):
        reads = [in_.buf]
        writes = [out.buf]
        k = e.dnext
        e.dnext = (k + 1) % len(e.dsems)
        sem = e.dsems[k]
        deps = self._deps(reads, writes, acc)
        if e.dcnt[k] > 0:
            deps[id(sem)] = (sem, e.dcnt[k])
        self._wait(e, deps)
        ins = e.h.dma_start(out=out.ap, in_=in_.ap, **kw)
        e.dcnt[k] += 16
        ins.then_inc(sem, 16)
        self._commit({id(sem): (sem, e.dcnt[k])}, reads, writes, acc)
        return ins

    def pad(self):
        import os
        scr = self.sb("padscr", [128, 8], F32)
        for nm, e in (("DVE", self.dve), ("ACT", self.act), ("POOL", self.pool)):
            for _ in range(int(os.environ.get("PAD_" + nm, "0"))):
                self.MEMSET(e, scr[:], 0.0) if e is not self.act else self.ACT(scr[:], scr[:], AF.Copy)
        for _ in range(int(os.environ.get("PAD_SP", "0"))):
            self.sp.h.wait_ge(self.sp.sem, 0)
        for _ in range(int(os.environ.get("PAD_PE", "0"))):
            self.pe.h.wait_ge(self.pe.sem, 0)

    def finish(self):
        self._finish_body()
        self.barrier()
        self.nc.all_engine_barrier()
        for sm_ in self.all_sems:
            self.nc.gpsimd.sem_clear(sm_)
        self.nc.all_engine_barrier()

    def _finish_body(self):
        self.pad()
        import os
        if os.environ.get("TAIL", "0") == "1":
            self.barrier()
            scr = self.sb("tailscr", [128, 64], F32)
            for _ in range(8):
                self.MEMSET(self.dve, scr[:], 0.0)
                self.ACT(scr[:], scr[:], AF.Copy)
                self.MEMSET(self.pool, scr[:], 0.0)
            self.barrier()
        for e in (self.sp, self.pool):
            for k, sem in enumerate(e.dsems):
                if e.dcnt[k] > 0:
                    self._wait(self.sp, {id(sem): (sem, e.dcnt[k])})
        for e in (self.pe, self.act, self.dve, self.pool):
            if e.cnt > 0:
                self._wait(self.sp, {id(e.sem): (e.sem, e.cnt)})

    def name(self, base):
        self._uid += 1
        return "%s_%d" % (base, self._uid)

    def sb(self, name, shape, dt):
        n = self.name(name)
        return Buf(self.stacks[-1].enter_context(self.nc.sbuf_tensor(n, list(shape), dt)), n)

    def ps(self, name, shape, dt=F32):
        n = self.name(name)
        return Buf(self.stacks[-1].enter_context(self.nc.psum_tensor(n, list(shape), dt)), n)

    def barrier(self):
        engs = (self.pe, self.act, self.dve, self.pool, self.sp)
        for e in engs:
            deps = {}
            for o in engs:
                if o is not e and o.cnt > 0:
                    deps[id(o.sem)] = (o.sem, o.cnt)
                for k, sem in enumerate(o.dsems):
                    if o.dcnt[k] > 0:
                        deps[id(sem)] = (sem, o.dcnt[k])
            self._wait(e, deps)

    @contextlib.contextmanager
    def scope(self):
        st = contextlib.ExitStack()
        self.stacks.append(st)
        try:
            yield
        finally:
            self.barrier()
            self.stacks.pop()
            st.close()

    def dram(self, name, shape, dt, kind="Internal"):
        return Buf(self.nc.dram_tensor(name, list(shape), dt, kind=kind).ap(), name)

    def ACT(self, out, in_, func, bias=0.0, scale=1.0, accum=None, acc=False):
        rd = [in_, bias, scale]
        wr = [out] + ([accum] if accum is not None else [])
        kw = {}
        if accum is not None:
            kw["accum_out"] = accum.ap
        if not (isinstance(bias, float) and bias == 0.0):
            kw["bias"] = _ap(bias)
        if not (isinstance(scale, float) and scale == 1.0):
            kw["scale"] = _ap(scale)
        return self.op(self.act, lambda h: h.activation(out=out.ap, in_=in_.ap, func=func, **kw), rd, wr, acc=acc)

    def TS(self, e, out, in0, s1, s2, op0, op1=None, accum=None, acc=False):
        kw = {}
        if op1 is not None:
            kw["op1"] = op1
        if accum is not None:
            kw["accum_out"] = accum.ap
        wr = [out] + ([accum] if accum is not None else [])
        return self.op(e, lambda h: h.tensor_scalar(out=out.ap, in0=in0.ap, scalar1=_ap(s1), scalar2=_ap(s2), op0=op0, **kw),
                       [in0, s1, s2], wr, acc=acc)

    def TT(self, e, out, in0, in1, op, acc=False):
        return self.op(e, lambda h: h.tensor_tensor(out=out.ap, in0=in0.ap, in1=in1.ap, op=op), [in0, in1], [out], acc=acc)

    def STT(self, e, out, in0, scalar, in1, op0, op1, accum=None, acc=False):
        kw = {}
        if accum is not None:
            kw["accum_out"] = accum.ap
        wr = [out] + ([accum] if accum is not None else [])
        return self.op(e, lambda h: h.scalar_tensor_tensor(out=out.ap, in0=in0.ap, scalar=_ap(scalar), in1=in1.ap, op0=op0,
                                                           op1=op1, **kw), [in0, scalar, in1], wr, acc=acc)

    def CP(self, e, out, in_, acc=False):
        if e is self.act:
            return self.ACT(out, in_, AF.Copy, acc=acc)
        return self.op(e, lambda h: h.tensor_copy(out=out.ap, in_=in_.ap), [in_], [out], acc=acc)

    def anycp(self, out, in_, acc=False):
        self._flip ^= 1
        return self.CP(self.act if self._flip else self.dve, out, in_, acc=acc)

    def MM(self, out, lhsT, rhs, start=True, stop=True):
        return self.op(self.pe, lambda h: h.matmul(out.ap, lhsT=lhsT.ap, rhs=rhs.ap, start=start, stop=stop),
                       [lhsT, rhs], [out], acc=not start)

    def TR(self, out, in_, ident):
        return self.op(self.pe, lambda h: h.transpose(out.ap, in_.ap, ident.ap), [in_, ident], [out])

    def RED(self, e, out, in_, op, negate=False):
        return self.op(e, lambda h: h.tensor_reduce(out=out.ap, in_=in_.ap, axis=AX.X, op=op, negate=negate), [in_], [out])

    def RECIP(self, out, in_):
        return self.op(self.dve, lambda h: h.reciprocal(out=out.ap, in_=in_.ap), [in_], [out])

    def MEMSET(self, e, out, val):
        return self.op(e, lambda h: h.memset(out.ap, val), [], [out])

    def ASEL(self, out, pattern, cmp, fill, base, cm):
        return self.op(self.pool, lambda h: h.affine_select(out=out.ap, in_=out.ap, pattern=pattern, compare_op=cmp,
                                                            fill=fill, base=base, channel_multiplier=cm), [out], [out])


class Rot:
    def __init__(self, mk, n):
        self.b = [mk(i) for i in range(n)]
        self.i = 0

    def get(self):
        b = self.b[self.i]
        self.i = (self.i + 1) % len(self.b)
        return b


class Consts:
    def __init__(self, P):
        self.P = P
        mk = lambda n, dt: P.sb(n, [128, 128], dt)
        self.ident_f = mk("ident_f", F32)
        self.ident_b = mk("ident_b", BF16)
        self.U = mk("U", F32)
        self.SU = mk("SU", F32)
        self.ONES = mk("ONES", F32)
        self.MASKT = mk("MASKT", F32)
        self.NEGM = mk("NEGM", F32)
        self.ELAST = mk("ELAST", F32)
        for t in (self.ident_f, self.ident_b):
            P.MEMSET(P.pool, t[:], 1.0)
            P.ASEL(t[:], [[-1, 128]], ALU.is_equal, 0.0, 0, 1)
        for t in (self.U, self.MASKT):
            P.MEMSET(P.pool, t[:], 1.0)
            P.ASEL(t[:], [[1, 128]], ALU.is_ge, 0.0, 0, -1)
        P.MEMSET(P.pool, self.SU[:], 1.0)
        P.ASEL(self.SU[:], [[-1, 128]], ALU.is_gt, 0.0, 0, 1)
        P.MEMSET(P.pool, self.ONES[:], 1.0)
        P.MEMSET(P.pool, self.NEGM[:], 0.0)
        P.ASEL(self.NEGM[:], [[-1, 128]], ALU.is_ge, -1e30, 0, 1)
        self.U_b = mk("U_b", BF16)
        self.SU_b = mk("SU_b", BF16)
        self.ONES_b = mk("ONES_b", BF16)
        self.ELAST_b = mk("ELAST_b", BF16)
        P.MEMSET(P.pool, self.ELAST[:], 1.0)
        P.ASEL(self.ELAST[:], [[0, 128]], ALU.is_equal, 0.0, -127, 1)
        for src, dst in ((self.U, self.U_b), (self.SU, self.SU_b), (self.ONES, self.ONES_b), (self.ELAST, self.ELAST_b)):
            P.CP(P.pool, dst[:], src[:])


def cast_dram(P, src, dst, rows, cols):
    sv = src.t if len(src.t.shape) == 2 else src.t.rearrange("a p k n -> (a p) (k n)")
    dv = dst.t if len(dst.t.shape) == 2 else dst.t.rearrange("a p k n -> (a p) (k n)")
    cstep = min(cols, 8192)
    for r in range(0, rows, 128):
        rr = min(128, rows - r)
        for c in range(0, cols, cstep):
            ce = min(c + cstep, cols)
            P.dma(P.pool, dst.view(dv[r:r + rr, c:ce]), src.view(sv[r:r + rr, c:ce]), acc=True)


def gemm_fm(P, aT, K, T, wl, nch, outT, ncols, psum, TT=512):
    KC = K // 128
    TT = min(TT, T)
    xts = Rot(lambda i: P.sb("g_xt", [128, KC, TT], BF16), 2)
    wts = Rot(lambda i: P.sb("g_wt", [128, KC, 128], BF16), 3)
    sts = Rot(lambda i: P.sb("g_st", [128, TT], F32), 3)
    aTv = aT.t.rearrange("(kc p) t -> p kc t", p=128)
    for tt in range(T // TT):
        xt = xts.get()
        half = KC // 2
        P.dma(P.sp, xt[:, 0:half, :], aT.view(aTv[:, 0:half, tt * TT:(tt + 1) * TT]), acc=True)
        P.dma(P.sp, xt[:, half:KC, :], aT.view(aTv[:, half:KC, tt * TT:(tt + 1) * TT]), acc=True)
        for n in range(nch):
            cols = min(128, ncols - n * 128)
            wt = wts.get()
            P.dma(P.sp, wt[:], wl[n])
            ps = psum.get()
            for kc in range(KC):
                P.MM(ps[0:cols, 0:TT], wt[:, kc, 0:cols], xt[:, kc, :], start=(kc == 0), stop=(kc == KC - 1))
            st = sts.get()
            P.anycp(st[0:cols, :], ps[0:cols, 0:TT])
            P.dma(P.sp, outT[n * 128:n * 128 + cols, tt * TT:(tt + 1) * TT], st[0:cols, :], acc=True)


def gemm_tm(P, aT, K, T, wl, nblk, out, psum, odt=F32):
    KC = K // 128
    TT = min(512, T)
    NB = TT // 128
    xts = Rot(lambda i: P.sb("t_xt", [128, KC, TT], BF16), 2)
    wts = Rot(lambda i: P.sb("t_wt", [128, KC // 4, 512], BF16), 3)
    sts = Rot(lambda i: P.sb("t_st", [128, 512], odt), 3)
    aTv = aT.t.rearrange("(kc p) t -> p kc t", p=128)
    KG = KC // 4
    for tt in range(T // TT):
        xt = xts.get()
        half = KC // 2
        P.dma(P.sp, xt[:, 0:half, :], aT.view(aTv[:, 0:half, tt * TT:(tt + 1) * TT]), acc=True)
        P.dma(P.sp, xt[:, half:KC, :], aT.view(aTv[:, half:KC, tt * TT:(tt + 1) * TT]), acc=True)
        for n in range(nblk):
            pss = [psum.get() for _ in range(NB)]
            for kg in range(4):
                wt = wts.get()
                P.dma(P.sp, wt[:], wl[n, :, kg * KG:(kg + 1) * KG, :])
                for b in range(NB):
                    for k2 in range(KG):
                        kc = kg * KG + k2
                        P.MM(pss[b][:, :], xt[:, kc, b * 128:(b + 1) * 128], wt[:, k2, :], start=(kc == 0), stop=(kc == KC - 1))
            for b in range(NB):
                st = sts.get()
                P.anycp(st[:], pss[b][:, :])
                t0 = tt * TT + b * 128
                P.dma(P.sp, out[t0:t0 + 128, n * 512:(n + 1) * 512], st[:], acc=True)


NFM0 = 2832
NCH0 = 23


def split_bf16(P, src, pieces, tmps):
    cur = src
    for i, pc in enumerate(pieces):
        P.CP(P.act, pc, cur)
        if i + 1 < len(pieces):
            P.TT(P.dve, tmps[i], cur, pc, ALU.subtract)
            cur = tmps[i]


def silu_gate(P, out, x, tmp):
    P.ACT(tmp, x, AF.Exp, scale=-1.0)
    P.TS(P.dve, tmp, tmp, 1.0, None, ALU.add)
    P.RECIP(tmp, tmp)
    P.TT(P.dve, out, x, tmp, ALU.mult)


def mem_prep(P, C, memT, wk, wv, psum):
    KC = D // 128
    mt = P.sb("memT", [128, KC, 256], BF16)
    mv = memT.t.rearrange("(kc p) m -> p kc m", p=128)
    P.dma(P.pool, mt[:, 0:16, :], memT.view(mv[:, 0:16, :]), acc=True)
    P.dma(P.pool, mt[:, 16:32, :], memT.view(mv[:, 16:32, :]), acc=True)
    wkt = P.sb("wkt", [128, KC, 128], BF16)
    wvt = P.sb("wvt", [128, KC, 128], BF16)
    P.dma(P.pool, wkt[:], wk[:])
    P.dma(P.pool, wvt[:], wv[:])
    KT = P.sb("memKT", [128, 256], BF16)
    V = P.sb("memV", [128, 2, 128], BF16)
    ps = psum.get()
    for kc in range(KC):
        P.MM(ps[:, 0:256], wkt[:, kc, :], mt[:, kc, :], start=(kc == 0), stop=(kc == KC - 1))
    P.CP(P.dve, KT[:], ps[:, 0:256])
    for mc in range(2):
        ps = psum.get()
        for kc in range(KC):
            P.MM(ps[:, 0:128], mt[:, kc, mc * 128:(mc + 1) * 128], wvt[:, kc, :], start=(kc == 0), stop=(kc == KC - 1))
        P.CP(P.dve, V[:, mc, :], ps[:, 0:128])
    return KT, V


def mem_attn_block(P, C, KT, V, mqT, mgT, outrows, t0, psum, W):
    q32 = W["ma_q32"].get()
    g32 = W["ma_g32"].get()
    P.dma(P.sp, q32[:], mqT[:, t0:t0 + 128])
    P.dma(P.sp, g32[:], mgT[:, t0:t0 + 128])
    qb = W["ma_qb"].get()
    P.CP(P.dve, qb[:], q32[:])
    ps = psum.get()
    P.MM(ps[:, 0:256], qb[:], KT[:])
    nmx = W["ma_v1"].get()
    P.RED(P.dve, nmx[:], ps[:, 0:256], ALU.max, negate=True)
    sc = 128 ** -0.5
    P.TS(P.dve, nmx[:], nmx[:], sc, None, ALU.mult)
    e = W["ma_e"].get()
    rs = W["ma_v2"].get()
    P.ACT(e[:], ps[:, 0:256], AF.Exp, bias=nmx[:], scale=sc, accum=rs[:])
    P.RECIP(rs[:], rs[:])
    pb = W["ma_pb"].get()
    P.TS(P.dve, pb[:], e[:], rs[:], None, ALU.mult)
    pT = W["ma_pT"].get()
    for mc in range(2):
        pst = W["psb"].get()
        P.TR(pst[:, 0:128], pb[:, mc * 128:(mc + 1) * 128], C.ident_b[:])
        P.anycp(pT[:, mc, :], pst[:, 0:128])
    ps2 = psum.get()
    for mc in range(2):
        P.MM(ps2[:, 0:128], V[:, mc, :], pT[:, mc, :], start=(mc == 0), stop=(mc == 1))
    gate = W["ma_gate"].get()
    silu_gate(P, gate[:], g32[:], W["ma_tmp"].get()[:])
    ob = W["ma_ob"].get()
    P.TT(P.dve, ob[:], ps2[:, 0:128], gate[:], ALU.mult)
    P.dma(P.sp, outrows[:, t0:t0 + 128], ob[:])


def lin_attn_core(P, C, psum, W, qtil, ktil, kdec, v, elast, emid, S32, Sbf, NDK, DV):
    for dc in range(NDK):
        P.TS(P.dve, Sbf[:, dc, :], S32[:, dc, :], emid[:, dc:dc + 1], None, ALU.mult)
    psA = psum.get()
    for dc in range(NDK):
        P.MM(psA[:, 0:128], ktil[:, dc, :], qtil[:, dc, :], start=(dc == 0), stop=(dc == NDK - 1))
    AT = W["AT"].get()
    P.TT(P.dve, AT[:], psA[:, 0:128], C.MASKT[:], ALU.mult)
    pso = psum.get()
    P.MM(pso[:, 0:DV], AT[:], v, start=True, stop=False)
    for dc in range(NDK):
        P.MM(pso[:, 0:DV], qtil[:, dc, :], Sbf[:, dc, :], start=False, stop=(dc == NDK - 1))
    for dc in range(NDK):
        psS = psum.get()
        P.MM(psS[:, 0:DV], kdec[:, dc * 128:(dc + 1) * 128], v)
        P.STT(P.dve, S32[:, dc, :], S32[:, dc, :], elast[:, dc:dc + 1], psS[:, 0:DV], ALU.mult, ALU.add)
    return pso


def rms_out(P, C, W, pso, DV, norm_g, gT_dram, outrows, t0, tag):
    nchunk = DV // 128
    junk = W[tag + "junk"].get()
    ss = W[tag + "v1"].get()
    o32 = W[tag + "o32"].get()
    P.CP(P.act, o32[:, 0:DV], pso[:, 0:DV])
    P.STT(P.dve, junk[:, 0:DV], o32[:, 0:DV], 1.0, o32[:, 0:DV], ALU.mult, ALU.mult, accum=ss[:])
    P.ACT(ss[:], ss[:], AF.Ln, bias=C.eps6[:], scale=1.0 / DV)
    P.ACT(ss[:], ss[:], AF.Exp, scale=-0.5)
    on = W[tag + "on"].get()
    P.TS(P.dve, on[:, 0:DV], o32[:, 0:DV], ss[:], None, ALU.mult)
    g32 = W[tag + "g32"].get()
    P.dma(P.sp, g32[:, 0:nchunk, :], gT_dram[:, :, t0:t0 + 128])
    gate = W[tag + "gate"].get()
    silu_gate(P, gate[:, 0:nchunk, :], g32[:, 0:nchunk, :], W[tag + "gtmp"].get()[:, 0:nchunk, :])
    pst = W["psb"].get()
    ob = W[tag + "ob"].get()
    for c in range(nchunk):
        P.TR(pst[:, c * 128:(c + 1) * 128], on[:, c * 128:(c + 1) * 128], C.ident_b[:])
    for c in range(nchunk):
        P.STT(P.dve, ob[:, c, :], pst[:, c * 128:(c + 1) * 128], norm_g[:, c:c + 1], gate[:, c, :], ALU.mult, ALU.mult)
    P.dma(P.sp, outrows[:, 0:nchunk, t0:t0 + 128], ob[:, 0:nchunk, :])


def phase_l0a(P, C, T, io, psum, stop_after=None):
    hT = io["hT"]
    hv = io["hv"]
    cast_dram(P, io["xT"], io["xTb"], D, T)
    if not io.get("precast"):
        cast_dram(P, io["w_fm"], io["w_fmb"], NCH0 * 128, 4096)
        cast_dram(P, io["w_tm"], io["w_tmb"], 2 * 128, 32 * 512)
    with P.scope():
        gemm_fm(P, io["xTb"], D, T, io["w_fmb"], NCH0, hT, NFM0, psum)
    if stop_after == "fm":
        return
    with P.scope():
        gemm_tm(P, io["xTb"], D, T, io["w_tmb"], 2, hv, psum, odt=BF16)
    if stop_after == "tm":
        return

    wa2 = P.sb("wa2", [16, 256], F32)
    ba = P.sb("ba", [1, 256], F32)
    P.dma(P.sp, wa2[:], io["w_a2"][:, :])
    P.dma(P.sp, ba[:], io["b_a"][:, :])
    ones1 = P.sb("ones1", [1, 128], F32)
    P.MEMSET(P.dve, ones1[:], 1.0)
    C.eps6 = P.sb("eps6", [128, 1], F32)
    P.MEMSET(P.dve, C.eps6[:], 1e-6)
    gng = P.sb("gng", [128, 4], F32)
    hng = P.sb("hng", [128, 1], F32)
    P.dma(P.sp, gng[:], io["gla_ng"][:, :])
    P.dma(P.sp, hng[:], io["hgrn_ng"][:, :])
    lg = P.sb("lg", [128, 3, 4], F32)
    P.dma(P.sp, lg[:], io["lb_logits"][:, :, :])
    P.ACT(lg[:], lg[:], AF.Exp)
    lsum = P.sb("lsum", [128, 4], F32)
    P.TT(P.dve, lsum[:], lg[:, 0, :], lg[:, 1, :], ALU.add)
    P.TT(P.dve, lsum[:], lsum[:], lg[:, 2, :], ALU.add)
    P.RECIP(lsum[:], lsum[:])
    lb = P.sb("lb", [128, 4], F32)
    P.TT(P.dve, lb[:], lg[:, 0, :], lsum[:], ALU.mult)
    omlb = P.sb("omlb", [128, 4], F32)
    P.TS(P.dve, omlb[:], lb[:], -1.0, 1.0, ALU.mult, ALU.add)

    KT, V = mem_prep(P, C, io["memT"], io["w_mk"], io["w_mv"], psum)

    W = {}
    rot = lambda name, shape, dt, n=2: Rot(lambda i: P.sb(name, shape, dt), n)
    W["psb"] = C.psb
    for nm, shape, dt in [("ma_q32", [128, 128], F32), ("ma_g32", [128, 128], F32), ("ma_qb", [128, 128], BF16),
                          ("ma_v1", [128, 1], F32), ("ma_v2", [128, 1], F32), ("ma_e", [128, 256], F32),
                          ("ma_pb", [128, 256], BF16), ("ma_pT", [128, 2, 128], BF16), ("ma_gate", [128, 128], F32),
                          ("ma_tmp", [128, 128], F32), ("ma_ob", [128, 128], BF16), ("AT", [128, 128], BF16)]:
        W[nm] = rot(nm, shape, dt)
    for tag, DV in (("g", 512), ("h", 128)):
        nch = DV // 128
        W[tag + "junk"] = rot(tag + "junk", [128, DV], F32, 1)
        W[tag + "o32"] = rot(tag + "o32", [128, DV], F32)
        W[tag + "v1"] = rot(tag + "v1", [128, 1], F32)
        W[tag + "on"] = rot(tag + "on", [128, DV], BF16)
        W[tag + "g32"] = rot(tag + "g32", [128, nch, 128], F32)
        W[tag + "gate"] = rot(tag + "gate", [128, nch, 128], F32)
        W[tag + "gtmp"] = rot(tag + "gtmp", [128, nch, 128], F32, 1)
        W[tag + "ob"] = rot(tag + "ob", [128, nch, 128], BF16)

    gS32 = P.sb("gS32", [128, 2, 512], F32)
    gSbf = P.sb("gSbf", [128, 2, 512], BF16)
    hS32 = [P.sb("hS32", [128, 1, 128], F32) for _ in range(4)]
    hSbf = [P.sb("hSbf", [128, 1, 128], BF16) for _ in range(4)]
    for t in [gSbf] + hSbf:
        P.MEMSET(P.pool, t[:], 0.0)
    P.dma(P.pool, gS32[:], io["gS_in"][:, :, :])
    for hd in range(4):
        P.dma(P.pool, hS32[hd][:], io["hS_in"][hd])

    wa2b = P.sb("wa2b", [16, 256], BF16)
    bab = P.sb("bab", [1, 256], BF16)
    ones1b = P.sb("ones1b", [1, 128], BF16)
    P.CP(P.dve, wa2b[:], wa2[:]); P.CP(P.dve, bab[:], ba[:]); P.CP(P.dve, ones1b[:], ones1[:])
    r_gab = rot("gab", [16, 128], BF16)
    r_lp = [rot("glp%d" % i, [128, 256], BF16) for i in range(3)]
    r_ltmp = [rot("gltmp%d" % i, [128, 256], F32, 1) for i in range(2)]
    r_kp = [rot("gkp%d" % i, [128, 2, 128], BF16) for i in range(2)]
    r_ktmp = rot("gktmp", [128, 2, 128], F32, 1)
    r_gq = rot("gq32", [128, 2, 128], F32)
    r_gk = rot("gk32", [128, 2, 128], F32)
    r_ga = rot("ga32", [16, 128], F32)
    r_gv = rot("gvb", [128, 512], BF16)
    r_l = rot("gl", [128, 256], F32)
    r_eb = rot("geb", [128, 2, 128], F32)
    r_ei = rot("gei", [128, 2, 128], F32)
    r_bp = rot("gbp", [128, 2], F32)
    r_bn = rot("gbn", [128, 2], F32)
    r_el = rot("gel", [128, 2], F32)
    r_em = rot("gem", [128, 2], F32)
    r_hem = rot("hem", [128, 4], F32)
    r_stmp = rot("hstmp", [128, 4, 128], F32, 1)
    r_qt = rot("gqt", [128, 2, 128], BF16)
    r_kt = rot("gkt", [128, 2, 128], BF16)
    r_ed = rot("ged", [128, 256], F32)
    r_kd = rot("gkd", [128, 256], BF16)
    r_gbp = [rot("hgbp%d" % i, [128, 4, 128], BF16) for i in range(3)]
    r_gbt = [rot("hgbt%d" % i, [128, 4, 128], F32, 1) for i in range(2)]
    r_gtp = [rot("hgtp%d" % i, [128, 4, 128], BF16) for i in range(3)]
    r_op = [rot("hop%d" % i, [128, 4, 128], BF16) for i in range(2)]
    r_hq = rot("hq32", [128, 4, 128], F32)
    r_hf = rot("hf32", [128, 4, 128], F32)
    r_hv = rot("hvb", [128, 512], BF16)
    r_t1 = rot("ht1", [128, 4, 128], F32)
    r_fT = rot("hfT", [128, 4, 128], F32)
    r_gT = rot("hgT", [128, 4, 128], F32)
    r_gtok = rot("hgtok", [128, 4, 128], F32)
    r_heb = rot("heb", [128, 4, 128], F32)
    r_hei = rot("hei", [128, 4, 128], F32)
    r_hbp = rot("hbp", [128, 4], F32)
    r_hbn = rot("hbn", [128, 4], F32)
    r_hel = rot("hel", [128, 4], F32)
    r_hqt = rot("hqt", [128, 4, 128], BF16)
    r_hkt = rot("hkt", [128, 4, 128], BF16)
    r_hed = rot("hed", [128, 4, 128], F32)
    r_hft = rot("hft", [128, 4, 128], F32)
    r_hkd = rot("hkd", [128, 4, 128], BF16)

    hTt = hT.t
    gqT = hT.view(hTt[0:256, :].rearrange("(c p) t -> p c t", p=128))
    gkT = hT.view(hTt[256:512, :].rearrange("(c p) t -> p c t", p=128))
    ggT = hT.view(hTt[512:1024, :].rearrange("(c p) t -> p c t", p=128))
    hqT = hT.view(hTt[1024:1536, :].rearrange("(c p) t -> p c t", p=128))
    hfT = hT.view(hTt[1536:2048, :].rearrange("(c p) t -> p c t", p=128))
    hgT = hT.view(hTt[2048:2560, :].rearrange("(c p) t -> p c t", p=128))
    mqT = hT[2560:2688, :]
    mgT = hT[2688:2816, :]
    gaT = hT[2816:2832, :]
    outT = io["outT"]
    out_g = outT.view(outT.t[0:512, :].rearrange("(c p) t -> p c t", p=128))
    out_h = [outT.view(outT.t[512 + 128 * hd:640 + 128 * hd, :].rearrange("(c p) t -> p c t", p=128)) for hd in range(4)]
    out_m = outT[1024:1152, :]

    if stop_after == "prep":
        return
    for ck in range(T // 128):
        t0 = ck * 128
        mem_attn_block(P, C, KT, V, mqT, mgT, out_m, t0, psum, W)
        if stop_after == "mem":
            continue

        gq = r_gq.get(); gk = r_gk.get(); ga = r_ga.get(); gv = r_gv.get()
        P.dma(P.sp, gq[:], gqT[:, :, t0:t0 + 128])
        P.dma(P.sp, gk[:], gkT[:, :, t0:t0 + 128])
        P.dma(P.sp, ga[:], gaT[:, t0:t0 + 128])
        P.dma(P.sp, gv[:], hv[t0:t0 + 128, 0:512])
        gab = r_gab.get()
        P.CP(P.act, gab[:], ga[:])
        psz = psum.get()
        P.MM(psz[:, 0:256], gab[:], wa2b[:], start=True, stop=False)
        P.MM(psz[:, 0:256], ones1b[:], bab[:], start=False, stop=True)
        l = r_l.get()
        P.ACT(l[:], psz[:, 0:256], AF.Exp, scale=-1.0)
        P.ACT(l[:], l[:], AF.Ln, bias=1.0)
        lp = [r.get() for r in r_lp]
        lt = [r.get() for r in r_ltmp]
        split_bf16(P, l[:], [x[:] for x in lp], [x[:] for x in lt])
        eb = r_eb.get(); ei = r_ei.get(); bp = r_bp.get(); bn = r_bn.get(); el = r_el.get(); em = r_em.get()
        qt = r_qt.get(); kt = r_kt.get()
        for dc in range(2):
            psb_ = psum.get()
            for p_ in range(3):
                P.MM(psb_[:, 0:128], lp[p_][:, dc * 128:(dc + 1) * 128], C.U_b[:], start=(p_ == 0), stop=(p_ == 2))
            P.TS(P.dve, bp[:, dc:dc + 1], psb_[:, 63:64], 1.0 / GLA_TAU, None, ALU.mult)
            P.TS(P.dve, bn[:, dc:dc + 1], psb_[:, 63:64], -1.0 / GLA_TAU, None, ALU.mult)
            P.ACT(eb[:, dc, :], psb_[:, 0:128], AF.Exp, bias=bp[:, dc:dc + 1], scale=-1.0 / GLA_TAU)
            P.ACT(ei[:, dc, :], psb_[:, 0:128], AF.Exp, bias=bn[:, dc:dc + 1], scale=1.0 / GLA_TAU)
            P.ACT(el[:, dc:dc + 1], psb_[:, 127:128], AF.Exp, scale=-1.0 / GLA_TAU)
            P.ACT(em[:, dc:dc + 1], psb_[:, 63:64], AF.Exp, scale=-1.0 / GLA_TAU)
            P.STT(P.dve, qt[:, dc, :], gq[:, dc, :], 256 ** -0.5, eb[:, dc, :], ALU.mult, ALU.mult)
            P.TT(P.dve, kt[:, dc, :], gk[:, dc, :], ei[:, dc, :], ALU.mult)
        psr = psum.get()
        for p_ in range(3):
            P.MM(psr[:, 0:256], C.SU_b[:], lp[p_][:], start=(p_ == 0), stop=(p_ == 2))
        ed = r_ed.get()
        P.ACT(ed[:], psr[:, 0:256], AF.Exp, scale=-1.0 / GLA_TAU)
        kp = [r.get() for r in r_kp]
        ktmp = r_ktmp.get()
        split_bf16(P, gk[:], [x[:] for x in kp], [ktmp[:]])
        pskt = psum.get()
        for dc in range(2):
            for p_ in range(2):
                P.MM(pskt[:, dc * 128:(dc + 1) * 128], kp[p_][:, dc, :], C.ident_b[:], start=(p_ == 0), stop=(p_ == 1))
        kd = r_kd.get()
        P.TT(P.dve, kd[:], pskt[:, 0:256], ed[:], ALU.mult)
        pso = lin_attn_core(P, C, psum, W, qt, kt, kd, gv[:], el, em, gS32, gSbf, 2, 512)
        rms_out(P, C, W, pso, 512, gng, ggT, out_g, t0, "g")
        if stop_after == "gla":
            continue

        hq = r_hq.get(); hf = r_hf.get(); hvb = r_hv.get()
        P.dma(P.sp, hq[:], hqT[:, :, t0:t0 + 128])
        P.dma(P.sp, hf[:], hfT[:, :, t0:t0 + 128])
        P.dma(P.sp, hvb[:], hv[t0:t0 + 128, 512:1024])
        t1 = r_t1.get(); fT = r_fT.get(); gT = r_gT.get()
        P.ACT(t1[:], hf[:], AF.Exp, scale=-1.0)
        P.TS(P.dve, t1[:], t1[:], 1.0, None, ALU.add)
        P.RECIP(t1[:], t1[:])
        for hd in range(4):
            P.TS(P.dve, fT[:, hd, :], t1[:, hd, :], omlb[:, hd:hd + 1], lb[:, hd:hd + 1], ALU.mult, ALU.add)
        P.ACT(gT[:], fT[:], AF.Ln)
        gbp = [r.get() for r in r_gbp]
        gbt = [r.get() for r in r_gbt]
        split_bf16(P, gT[:], [x[:] for x in gbp], [x[:] for x in gbt])
        gtp = [r.get() for r in r_gtp]
        for p_ in range(3):
            pst_ = C.psb.get()
            for hd in range(4):
                P.TR(pst_[:, hd * 128:(hd + 1) * 128], gbp[p_][:, hd, :], C.ident_b[:])
            P.anycp(gtp[p_][:], pst_[:, 0:512].rearrange("p (h d) -> p h d", h=4))
        heb = r_heb.get(); hei = r_hei.get(); hbp = r_hbp.get(); hbn = r_hbn.get(); hel = r_hel.get()
        hqt = r_hqt.get(); hkt = r_hkt.get(); hed = r_hed.get(); hkd = r_hkd.get(); hem = r_hem.get()
        t2 = r_t1.get()
        silu_gate(P, t2[:], hq[:], r_stmp.get()[:])
        for hd in range(4):
            psb4 = psum.get()
            for p_ in range(3):
                P.MM(psb4[:, 0:128], gtp[p_][:, hd, :], C.U_b[:], start=(p_ == 0), stop=(p_ == 2))
            b_ = psb4[:, 0:128]
            P.TS(P.dve, hbn[:, hd:hd + 1], b_[:, 63:64], -1.0, None, ALU.mult)
            P.CP(P.dve, hbp[:, hd:hd + 1], b_[:, 63:64])
            P.ACT(heb[:, hd, :], b_, AF.Exp, bias=hbn[:, hd:hd + 1], scale=1.0)
            P.ACT(hei[:, hd, :], b_, AF.Exp, bias=hbp[:, hd:hd + 1], scale=-1.0)
            P.ACT(hel[:, hd:hd + 1], b_[:, 127:128], AF.Exp)
            P.ACT(hem[:, hd:hd + 1], b_[:, 63:64], AF.Exp)
            psr4 = psum.get()
            for p_ in range(3):
                P.MM(psr4[:, 0:128], C.SU_b[:], gtp[p_][:, hd, :], start=(p_ == 0), stop=(p_ == 2))
            P.ACT(hed[:, hd, :], psr4[:, 0:128], AF.Exp)
        P.TT(P.dve, hqt[:], t2[:], heb[:], ALU.mult)
        P.TS(P.dve, fT[:], fT[:], -1.0, 1.0, ALU.mult, ALU.add)
        P.TT(P.dve, hkt[:], fT[:], hei[:], ALU.mult)
        op_ = [r.get() for r in r_op]
        otmp = r_stmp.get()
        split_bf16(P, fT[:], [x[:] for x in op_], [otmp[:]])
        for hd in range(4):
            pso_ = psum.get()
            for p_ in range(2):
                P.MM(pso_[:, 0:128], op_[p_][:, hd, :], C.ident_b[:], start=(p_ == 0), stop=(p_ == 1))
            P.TT(P.dve, hkd[:, hd, :], pso_[:, 0:128], hed[:, hd, :], ALU.mult)
        for hd in range(4):
            pso = lin_attn_core(P, C, psum, W, hqt[:, hd:hd + 1, :], hkt[:, hd:hd + 1, :], hkd[:, hd, :],
                                hvb[:, hd * 128:(hd + 1) * 128], hel[:, hd:hd + 1], hem[:, hd:hd + 1], hS32[hd], hSbf[hd], 1, 128)
            hgT_h = hT.view(hTt[2048 + 128 * hd:2176 + 128 * hd, :].rearrange("(c p) t -> p c t", p=128))
            rms_out(P, C, W, pso, 128, hng, hgT_h, out_h[hd], t0, "h")
    P.dma(P.pool, io["gS_out"][:, :, :], gS32[:])
    for hd in range(4):
        P.dma(P.pool, io["hS_out"][hd], hS32[hd][:])


def new_prog():
    nc = bass.Bass("TRN2", target_bir_lowering=False)
    P = Prog(nc)
    C = Consts(P)
    psum = Rot(lambda i: P.ps("psum", [128, 512], F32), 6)
    C.psb = Rot(lambda i: P.ps("psb", [128, 1024], BF16), 2)
    return nc, P, C, psum


def build_l0a(T, stop_after=None, debug=True, precast=False):
    nc, P, C, psum = new_prog()
    io = {}
    ext = lambda n, s, dt=F32: P.dram(n, s, dt, kind="ExternalInput")
    io["xT"] = ext("xT", [D, T])
    io["precast"] = precast
    if not precast:
        io["w_fm"] = ext("w_fm", [NCH0, 128, 32, 128])
        io["w_tm"] = ext("w_tm", [2, 128, 32, 512])
    io["w_a2"] = ext("w_a2", [16, 256])
    io["b_a"] = ext("b_a", [1, 256])
    io["gla_ng"] = ext("gla_ng", [128, 4])
    io["hgrn_ng"] = ext("hgrn_ng", [128, 1])
    io["lb_logits"] = ext("lb_logits", [128, 3, 4])
    io["memT"] = ext("memT", [D, 256])
    io["w_mk"] = ext("w_mk", [128, 32, 128])
    io["w_mv"] = ext("w_mv", [128, 32, 128])
    kd = "ExternalOutput" if debug else "Internal"
    io["hT"] = P.dram("hT", [NCH0 * 128, T], F32, kind=kd)
    io["hv"] = P.dram("hv", [T, 1024], BF16, kind=kd)
    io["xTb"] = P.dram("xTb", [D, T], BF16)
    wk_ = "ExternalInput" if precast else "ExternalOutput"
    io["w_fmb"] = P.dram("w_fmb", [NCH0, 128, 32, 128], BF16, kind=wk_)
    io["w_tmb"] = P.dram("w_tmb", [2, 128, 32, 512], BF16, kind=wk_)
    io["outT"] = P.dram("outT", [1152, T], BF16, kind="ExternalOutput")
    io["gS_in"] = ext("gS_in", [128, 2, 512])
    io["hS_in"] = ext("hS_in", [4, 128, 1, 128])
    io["gS_out"] = P.dram("gS_out", [128, 2, 512], F32, kind="ExternalOutput")
    io["hS_out"] = P.dram("hS_out", [4, 128, 1, 128], F32, kind="ExternalOutput")
    if stop_after == "h2":
        io["dbg"] = P.dram("dbg", [T // 128, 6, 128, 4, 128], F32, kind="ExternalOutput")
    phase_l0a(P, C, T, io, psum, stop_after)
    P.finish()
    return nc, P


def tile_w_fm(w, ncols_pad):
    K, N = w.shape
    wp = np.zeros((K, ncols_pad), np.float32)
    wp[:, :N] = w
    return np.ascontiguousarray(wp.reshape(K // 128, 128, ncols_pad // 128, 128).transpose(2, 1, 0, 3))


def tile_w_tm(w):
    K, N = w.shape
    return np.ascontiguousarray(w.reshape(K // 128, 128, N // 512, 512).transpose(2, 1, 0, 3))


def l0a_inputs(inp, b, g, T):
    w = inp["ev_w_in"][0]
    o = np.cumsum([0, 1024, 1024, 2048, 2048, 16, 2048, 2048, 2048, 2048, 512, 512])
    gq, gk, gv, gg, ga, hq, hf, hi, hg, mq, mg = [w[:, o[i]:o[i + 1]] for i in range(11)]
    sl = lambda m, width: m[:, g * width:(g + 1) * width]
    fm = np.concatenate([sl(gq, 256), sl(gk, 256), sl(gg, 512), sl(hq, 512), sl(hf, 512), sl(hg, 512), sl(mq, 128),
                         sl(mg, 128), ga], axis=1)
    tm = np.concatenate([sl(gv, 512), sl(hi, 512)], axis=1)
    d = {}
    d["xT"] = np.ascontiguousarray(inp["x"][b, :T].T)
    d["w_fm"] = tile_w_fm(fm, NCH0 * 128)
    d["w_tm"] = tile_w_tm(tm)
    d["w_a2"] = np.ascontiguousarray(inp["ev_gla_w_a2"][0][:, g * 256:(g + 1) * 256])
    d["b_a"] = np.ascontiguousarray(inp["ev_gla_b_a"][0][None, g * 256:(g + 1) * 256])
    d["gla_ng"] = np.ascontiguousarray(inp["ev_gla_norm_g"][0].reshape(4, 128).T)
    d["hgrn_ng"] = np.ascontiguousarray(inp["ev_hgrn_norm_g"][0].reshape(128, 1))
    lbl = inp["hgrn_lb_logits"][:, g * 512:(g + 1) * 512]
    d["lb_logits"] = np.ascontiguousarray(lbl.reshape(3, 4, 128).transpose(2, 0, 1))
    d["memT"] = np.ascontiguousarray(inp["mem"][b].T)
    tk = lambda m: np.ascontiguousarray(m[:, g * 128:(g + 1) * 128].reshape(32, 128, 128).transpose(1, 0, 2))
    d["w_mk"] = tk(inp["ev_mem_w_k"][0])
    d["w_mv"] = tk(inp["ev_mem_w_v"][0])
    return d


def phase_b(P, C, TB, K, io, psum, want_T):
    KC = K // 128
    KG = KC // 4
    NB = 2 if TB % 256 == 0 else 1
    TT = NB * 128
    catT = io["catT"]; xres = io["xres"]; xo = io["xo"]
    if not io.get("precast"):
        cast_dram(P, io["w_out"], io["w_outb"], 8 * 128, KC * 512)
    wl = io["w_outb"]
    gbc = P.sb("gbc", [128, 4096], F32)
    bbc = P.sb("bbc", [128, 4096], F32)
    P.dma(P.sp, gbc[:], io["ln_g"].view(io["ln_g"].t.partition_broadcast(128)))
    P.dma(P.sp, bbc[:], io["ln_b"].view(io["ln_b"].t.partition_broadcast(128)))
    eps5 = P.sb("eps5", [128, 1], F32)
    P.MEMSET(P.dve, eps5[:], 1e-5)
    aTs = Rot(lambda i: P.sb("b_aT", [128, KC, TT], BF16), 1)
    wts = Rot(lambda i: P.sb("b_wt", [128, KG, 512], BF16), 3)
    zs = Rot(lambda i: P.sb("b_z", [128, NB, 4096], F32), 1)
    xrs = Rot(lambda i: P.sb("b_xr", [128, 512], F32), 3)
    sts = Rot(lambda i: P.sb("b_stats", [128, 8, 6], F32), 2)
    mvs = Rot(lambda i: P.sb("b_mv", [128, 2], F32), 2)
    rss = Rot(lambda i: P.sb("b_rs", [128, 2], F32), 2)
    obs = Rot(lambda i: P.sb("b_ob", [128, 4096], BF16), 1)
    oTs = Rot(lambda i: P.sb("b_oT", [128, 8, 128], BF16), 2)
    cv = catT.t.rearrange("(kc p) t -> p kc t", p=128)
    for tt in range(TB // TT):
        aT = aTs.get()
        h2 = KC // 2
        P.dma(P.sp, aT[:, 0:h2, :], catT.view(cv[:, 0:h2, tt * TT:(tt + 1) * TT]), acc=True)
        P.dma(P.sp, aT[:, h2:KC, :], catT.view(cv[:, h2:KC, tt * TT:(tt + 1) * TT]), acc=True)
        z = zs.get()
        for n in range(8):
            pss = [psum.get() for _ in range(NB)]
            for kg in range(4):
                wt = wts.get()
                P.dma(P.sp, wt[:], wl[n, :, kg * KG:(kg + 1) * KG, :])
                for b in range(NB):
                    for k2 in range(KG):
                        kc = kg * KG + k2
                        P.MM(pss[b][:, :], aT[:, kc, b * 128:(b + 1) * 128], wt[:, k2, :], start=(kc == 0), stop=(kc == KC - 1))
            for b in range(NB):
                xr = xrs.get()
                t0 = tt * TT + b * 128
                P.dma(P.sp, xr[:], xres[t0:t0 + 128, n * 512:(n + 1) * 512])
                P.STT(P.dve, z[:, b, n * 512:(n + 1) * 512], xr[:], ALPHA, pss[b][:, :], ALU.mult, ALU.add, acc=True)
        for b in range(NB):
            t0 = tt * TT + b * 128
            st = sts.get(); mv = mvs.get(); rs = rss.get()
            for c in range(8):
                P.op(P.dve, lambda h, c=c: h.bn_stats(out=st[:, c, :].ap, in_=z[:, b, c * 512:(c + 1) * 512].ap), [z[:]], [st[:]], acc=True)
            P.op(P.dve, lambda h: h.bn_aggr(out=mv[:].ap, in_=st[:].ap), [st[:]], [mv[:]])
            P.ACT(rs[:, 0:1], mv[:, 1:2], AF.Ln, bias=eps5[:])
            P.ACT(rs[:, 0:1], rs[:, 0:1], AF.Exp, scale=-0.5)
            P.STT(P.dve, rs[:, 1:2], mv[:, 0:1], -1.0, rs[:, 0:1], ALU.mult, ALU.mult)
            P.TS(P.dve, z[:, b, :], z[:, b, :], rs[:, 0:1], rs[:, 1:2], ALU.mult, ALU.add)
            P.TT(P.pool, z[:, b, :], z[:, b, :], gbc[:], ALU.mult)
            P.TT(P.dve, z[:, b, :], z[:, b, :], bbc[:], ALU.add)
            P.dma(P.sp, xo[t0:t0 + 128, :], z[:, b, :], acc=True)
            if want_T:
                ob = obs.get()
                P.CP(P.act, ob[:], z[:, b, :])
                xoT = io["xoT"]
                xv = xoT.t.rearrange("(c p) t -> p c t", p=128)
                for q4 in range(4):
                    pst = C.psb.get()
                    for c8 in range(8):
                        c = q4 * 8 + c8
                        P.TR(pst[:, c8 * 128:(c8 + 1) * 128], ob[:, c * 128:(c + 1) * 128], C.ident_b[:])
                    oT = oTs.get()
                    P.anycp(oT[:], pst[:, :].rearrange("p (c t) -> p c t", c=8))
                    P.dma(P.sp, xoT.view(xv[:, q4 * 8:(q4 + 1) * 8, t0:t0 + 128]), oT[:], acc=True)


def build_b(TB, K, want_T, precast=False):
    nc, P, C, psum = new_prog()
    io = {}
    ext = lambda n, s, dt=F32: P.dram(n, s, dt, kind="ExternalInput")
    io["catT"] = ext("catT", [K, TB], BF16)
    io["precast"] = precast
    if not precast:
        io["w_out"] = ext("w_out", [8, 128, K // 128, 512])
    io["w_outb"] = P.dram("w_outb", [8, 128, K // 128, 512], BF16, kind="ExternalInput" if precast else "ExternalOutput")
    io["xres"] = ext("xres", [TB, 4096])
    io["ln_g"] = ext("ln_g", [1, 4096])
    io["ln_b"] = ext("ln_b", [1, 4096])
    io["xo"] = P.dram("xo", [TB, 4096], F32, kind="ExternalOutput")
    if want_T:
        io["xoT"] = P.dram("xoT", [4096, TB], BF16, kind="ExternalOutput")
    with P.scope():
        phase_b(P, C, TB, K, io, psum, want_T)
    P.finish()
    return nc, P


NFM1 = 4352
NCH1 = 34


def phase_l1a1(P, C, T, io, psum):
    h1T = io["h1T"]
    if not io.get("precast"):
        cast_dram(P, io["w_fm"], io["w_fmb"], NCH1 * 128, 4096)
    with P.scope():
        gemm_fm(P, io["xT"], D, T, io["w_fmb"], NCH1, h1T, NFM1, psum)
    TT = min(256, T)
    NB = TT // 128
    cw = P.sb("cw", [128, 16, 4], F32)
    cb = P.sb("cb", [128, 16], F32)
    sk = P.sb("sk", [128, 16], F32)
    P.dma(P.sp, cw[:], io["conv_w"][:, :, :])
    P.dma(P.sp, cb[:], io["conv_b"][:, :])
    P.dma(P.sp, sk[:], io["skip"][:, :])
    bd = {}
    for nm in ("bdq", "bdk", "bdv"):
        bd[nm] = P.sb(nm, [128, 16, 128], BF16)
        P.dma(P.pool, bd[nm][:], io[nm][:, :, :])
    wif = P.sb("wif", [128, 3, 16, 8], BF16)
    P.dma(P.pool, wif[:], io["w_if"][:, :, :, :])
    xms = Rot(lambda i: P.sb("xm", [128, 16, TT + 3], F32), 1)
    accs = Rot(lambda i: P.sb("acc", [128, 16, TT], F32), 1)
    tmps = Rot(lambda i: P.sb("ctmp", [128, 16, TT], F32), 1)
    xcbs = Rot(lambda i: P.sb("xcb", [128, 16, TT], BF16), 1)
    xmbs = Rot(lambda i: P.sb("xmb", [128, 16, TT], BF16), 1)
    sxcs = Rot(lambda i: P.sb("sxc", [128, 16, TT], BF16), 1)
    fmo = {nm: Rot(lambda i, nm=nm: P.sb("o_" + nm, [128, 16, TT], BF16), 1) for nm in ("q", "k", "v")}
    tmo = {nm: Rot(lambda i, nm=nm: P.sb("t_" + nm, [128, 2048], BF16), 2) for nm in ("k", "v")}
    gps = Rot(lambda i: P.sb("gp", [128, 8], F32), 2)
    xv = h1T.t[0:2048, :].rearrange("(c p) t -> p c t", p=128)
    dT = {nm: io[nm + "T"] for nm in ("q", "k")}
    dTv = {nm: dT[nm].t.rearrange("(c p) t -> p c t", p=128) for nm in dT}
    sxv = io["sxcT"].t.rearrange("(c p) t -> p c t", p=128)
    for tt in range(T // TT):
        t0 = tt * TT
        xm = xms.get()
        if tt == 0:
            P.dma(P.pool, xm[:, :, 0:3], io["halo_in"][:, :, :], acc=True)
            P.dma(P.sp, xm[:, 0:8, 3:TT + 3], h1T.view(xv[:, 0:8, 0:TT]), acc=True)
            P.dma(P.sp, xm[:, 8:16, 3:TT + 3], h1T.view(xv[:, 8:16, 0:TT]), acc=True)
        else:
            P.dma(P.sp, xm[:, 0:8, :], h1T.view(xv[:, 0:8, t0 - 3:t0 + TT]), acc=True)
            P.dma(P.sp, xm[:, 8:16, :], h1T.view(xv[:, 8:16, t0 - 3:t0 + TT]), acc=True)
        acc = accs.get()
        for c in range(16):
            P.TS(P.dve, acc[:, c, :], xm[:, c, 0:TT], cw[:, c, 0:1], cb[:, c:c + 1], ALU.mult, ALU.add, acc=True)
            for j in (1, 2, 3):
                P.STT(P.dve, acc[:, c, :], xm[:, c, j:j + TT], cw[:, c, j:j + 1], acc[:, c, :], ALU.mult, ALU.add, acc=True)
        xcb = xcbs.get(); xmb = xmbs.get(); sxc = sxcs.get(); tmp = tmps.get()
        silu_gate(P, acc[:], acc[:], tmp[:])
        P.CP(P.act, xcb[:], acc[:])
        P.CP(P.act, xmb[:], xm[:, :, 3:TT + 3])
        for c in range(16):
            P.TS(P.pool, sxc[:, c, :], acc[:, c, :], sk[:, c:c + 1], None, ALU.mult, acc=True)
        P.dma(P.sp, io["sxcT"].view(sxv[:, :, t0:t0 + TT]), sxc[:])
        outs = {}
        for nm, src, w in (("q", xcb, bd["bdq"]), ("k", xcb, bd["bdk"]), ("v", xmb, bd["bdv"])):
            o = fmo[nm].get()
            outs[nm] = o
            for c in range(16):
                ps = psum.get()
                P.MM(ps[:, 0:TT], w[:, c, :], src[:, c, :])
                P.anycp(o[:, c, :], ps[:, 0:TT], acc=True)
            if nm in dT:
                P.dma(P.sp, dT[nm].view(dTv[nm][:, :, t0:t0 + TT]), o[:])
        for b in range(NB):
            bs = slice(b * 128, (b + 1) * 128)
            for nm, src, w in (("k", xcb, bd["bdk"]), ("v", xmb, bd["bdv"])):
                o = tmo[nm].get()
                for c4 in range(4):
                    ps = psum.get()
                    for c in range(c4 * 4, c4 * 4 + 4):
                        P.MM(ps[:, (c % 4) * 128:(c % 4 + 1) * 128], src[:, c, bs], w[:, c, :])
                    P.anycp(o[:, c4 * 512:(c4 + 1) * 512], ps[:, :], acc=True)
                P.dma(P.sp, io[nm + "_tm"][t0 + b * 128:t0 + (b + 1) * 128, :], o[:], acc=True)
            ps = psum.get()
            n = 0
            for gi, nm in enumerate(("q", "k", "v")):
                for c in range(16):
                    P.MM(ps[:, 0:8], outs[nm][:, c, bs], wif[:, gi, c, :], start=(n == 0), stop=(n == 47))
                    n += 1
            gp = gps.get()
            P.CP(P.dve, gp[:], ps[:, 0:8])
            P.dma(P.sp, io["gp"][t0 + b * 128:t0 + (b + 1) * 128, :], gp[:], acc=True)
        if tt == T // TT - 1:
            P.dma(P.pool, io["halo_out"][:, :, :], xm[:, :, TT:TT + 3])


def build_l1a1(T, precast=False):
    nc, P, C, psum = new_prog()
    io = {}
    ext = lambda n, s, dt=F32: P.dram(n, s, dt, kind="ExternalInput")
    out = lambda n, s, dt=F32: P.dram(n, s, dt, kind="ExternalOutput")
    io["xT"] = ext("xT", [D, T], BF16)
    io["precast"] = precast
    if not precast:
        io["w_fm"] = ext("w_fm", [NCH1, 128, 32, 128])
    io["w_fmb"] = P.dram("w_fmb", [NCH1, 128, 32, 128], BF16, kind="ExternalInput" if precast else "ExternalOutput")
    io["conv_w"] = ext("conv_w", [128, 16, 4])
    io["conv_b"] = ext("conv_b", [128, 16])
    io["skip"] = ext("skip", [128, 16])
    for nm in ("bdq", "bdk", "bdv"):
        io[nm] = ext(nm, [128, 16, 128])
    io["w_if"] = ext("w_if", [128, 3, 16, 8])
    io["h1T"] = out("h1T", [NCH1 * 128, T])
    io["qT"] = out("qT", [2048, T], BF16)
    io["kT"] = out("kT", [2048, T], BF16)
    io["sxcT"] = out("sxcT", [2048, T], BF16)
    io["k_tm"] = out("k_tm", [T, 2048], BF16)
    io["v_tm"] = out("v_tm", [T, 2048], BF16)
    io["gp"] = out("gp", [T, 8])
    io["halo_in"] = ext("halo_in", [128, 16, 3])
    io["halo_out"] = out("halo_out", [128, 16, 3])
    phase_l1a1(P, C, T, io, psum)
    P.finish()
    return nc, P


def blockdiag(w):
    out = np.zeros((16, 128, 128), np.float32)
    wb = w.reshape(16, 32, 4, 4)
    for n in range(32):
        out[:, 4 * n:4 * n + 4, 4 * n:4 * n + 4] = wb[:, n]
    return np.ascontiguousarray(out.transpose(1, 0, 2))


def l1a1_inputs(inp, x1T_b, h, T):
    w = inp["od_w_in"][0]
    hs = slice(h * 2048, (h + 1) * 2048)
    fm = np.concatenate([w[:, 0:8192][:, hs], w[:, 8192:16384][:, hs], w[:, 16384 + h * 128:16384 + (h + 1) * 128],
                         w[:, 16896 + h * 128:16896 + (h + 1) * 128]], axis=1)
    d = {"xT": x1T_b, "w_fm": tile_w_fm(fm, NCH1 * 128)}
    pc = lambda v: np.ascontiguousarray(v[hs].reshape(16, 128).T)
    d["conv_w"] = np.ascontiguousarray(inp["od_conv_w"][0][:, hs].reshape(4, 16, 128).transpose(2, 1, 0))
    d["conv_b"] = pc(inp["od_conv_b"][0])
    d["skip"] = pc(inp["od_skip"][0])
    bs = slice(h * 512, (h + 1) * 512)
    d["bdq"] = blockdiag(inp["od_w_q"][0][bs]); d["bdk"] = blockdiag(inp["od_w_k"][0][bs]); d["bdv"] = blockdiag(inp["od_w_v"][0][bs])
    wif = inp["od_w_if"][0].reshape(3, 8192, 8)[:, hs]
    d["w_if"] = np.ascontiguousarray(wif.reshape(3, 16, 128, 8).transpose(2, 0, 1, 3))
    return d


def phase_l1a2(P, C, T, io, psum):
    h1T = io["h1T"]
    hTt = h1T.t
    KT = P.sb("memKT2", [128, 256], BF16)
    V = P.sb("memV2", [128, 2, 128], BF16)
    with P.scope():
        KT_, V_ = mem_prep(P, C, io["memT"], io["w_mk"], io["w_mv"], psum)
        P.CP(P.dve, KT[:], KT_[:])
        P.CP(P.dve, V[:], V_[:])
    W = {"psb": C.psb}
    rot = lambda name, shape, dt, n=2: Rot(lambda i: P.sb(name, shape, dt), n)
    for nm, shape, dt in [("ma_q32", [128, 128], F32), ("ma_g32", [128, 128], F32), ("ma_qb", [128, 128], BF16),
                          ("ma_v1", [128, 1], F32), ("ma_v2", [128, 1], F32), ("ma_e", [128, 256], F32),
                          ("ma_pb", [128, 256], BF16), ("ma_pT", [128, 2, 128], BF16), ("ma_gate", [128, 128], F32),
                          ("ma_tmp", [128, 128], F32), ("ma_ob", [128, 128], BF16)]:
        W[nm] = rot(nm, shape, dt)
    mqT = h1T[4096:4224, :]
    mgT = h1T[4224:4352, :]
    zTv = h1T.view(hTt[2048:4096, :].rearrange("(c p) t -> p c t", p=128))
    outT = io["outT"]
    out_x = outT.view(outT.t[0:2048, :].rearrange("(c p) t -> p c t", p=128))
    out_m = outT[2048:2176, :]
    fmv = lambda nm: io[nm].view(io[nm].t.rearrange("(c p) t -> p c t", p=128))
    qTv, kTv, sxv = fmv("qT"), fmv("kT"), fmv("sxcT")

    ng = P.sb("mhng", [128, 16], F32)
    P.dma(P.sp, ng[:], io["mh_ng"][:, :])
    b2 = P.sb("b2", [128, 2], F32)
    P.dma(P.sp, b2[:], io["b_if2"].view(io["b_if2"].t.partition_broadcast(128)))
    eps6 = P.sb("eps6b", [128, 1], F32)
    P.MEMSET(P.dve, eps6[:], 1e-6)
    EC = P.sb("EC", [128, 16, 16], BF16)
    P.MEMSET(P.pool, EC[:], 0.0)
    for c in range(16):
        P.MEMSET(P.pool, EC[:, c, c:c + 1], 1.0)
    Cb = P.sb("Cb", [128, 16, 2048], BF16)
    for c in range(0, 16, 4):
        P.dma(P.pool, Cb[:, c:c + 4, :], io["C_in"][:, c:c + 4, :], acc=True)
    nv32 = P.sb("nv32", [128, 16], F32)
    nvb = P.sb("nvb", [128, 16], BF16)
    mrun = P.sb("mrun", [128, 1], F32)
    P.dma(P.pool, nv32[:], io["nv_in"][:, :])
    P.dma(P.pool, mrun[:], io["m_in"][:, :])
    P.CP(P.dve, nvb[:], nv32[:])
    SC = 2048 ** -0.5

    r_q = rot("m_q", [128, 16, 128], BF16); r_k = rot("m_k", [128, 16, 128], BF16)
    r_sx = rot("m_sx", [128, 16, 128], BF16); r_z = rot("m_z", [128, 16, 128], F32)
    r_vt = rot("m_vt", [128, 2048], BF16); r_kt = rot("m_kt", [128, 2048], BF16)
    r_gp = rot("m_gp", [128, 4, 2], F32)
    r_g2 = rot("m_g2", [128, 2], F32)
    sm = {nm: rot("m_" + nm, [128, 1], F32) for nm in ("lf", "inter", "rmax", "mi", "nmi", "winter", "elim", "lw", "nmn", "wj",
                                                        "carry", "tcar", "rsum", "den", "rden", "s2", "bcol", "ccol")}
    r_R1 = rot("m_R1", [128, 128], F32, 1)
    r_lf3 = rot("m_lf3", [128, 3], BF16); r_c1t = rot("m_c1t", [128, 2], F32, 4); r_cb = rot("m_cb", [128, 1], BF16, 4)
    r_dg = [rot("m_dg%d" % i, [128, 128], BF16, 1) for i in range(3)]
    r_v6 = rot("m_v6", [128, 3, 2], BF16); r_v2t = rot("m_v2t", [128, 2, 2], F32); r_bc6 = rot("m_bc6", [128, 3, 2], F32)
    r_wi = rot("m_wi", [128, 128], F32, 1)
    r_v2 = rot("m_v2", [128, 2], F32); r_bc2 = rot("m_bc2", [128, 2], F32)
    r_sc = rot("m_sc", [128, 128], BF16); r_scT = rot("m_scT", [128, 128], BF16)
    r_h = rot("m_h", [128, 2048], F32, 1); r_t5 = rot("m_t5", [128, 512], F32, 2)
    r_hn = rot("m_hn", [128, 2048], BF16, 1)
    r_st = rot("m_st", [128, 4, 6], F32); r_mv = rot("m_mv", [128, 2], F32); r_rs = rot("m_rs", [128, 2], F32)
    r_gate = rot("m_gate", [128, 16, 128], F32, 1); r_gt = rot("m_gt", [128, 16, 128], F32, 1)
    r_o1 = rot("m_o1", [128, 16, 128], F32, 1); r_ob = rot("m_ob", [128, 16, 128], BF16, 2)
    r_kw = rot("m_kw", [128, 2048], BF16, 1)

    for ck in range(T // 128):
        t0 = ck * 128
        ts = slice(t0, t0 + 128)
        mem_attn_block(P, C, KT, V, mqT, mgT, out_m, t0, psum, W)
        q = r_q.get(); k = r_k.get(); sx = r_sx.get(); z = r_z.get(); vt = r_vt.get(); kt = r_kt.get(); gp = r_gp.get()
        P.dma(P.sp, q[:], qTv[:, :, ts]); P.dma(P.sp, k[:], kTv[:, :, ts]); P.dma(P.sp, sx[:], sxv[:, :, ts])
        P.dma(P.sp, z[:], zTv[:, :, ts])
        P.dma(P.sp, vt[:], io["v_tm"][ts, :]); P.dma(P.sp, kt[:], io["k_tm"][ts, :])
        P.dma(P.sp, gp[:], io["gp4"].view(io["gp4"].t[:, ts, :].rearrange("g t c -> t g c")))
        g2 = r_g2.get()
        P.TT(P.dve, g2[:], gp[:, 0, :], gp[:, 1, :], ALU.add)
        P.TT(P.dve, g2[:], g2[:], gp[:, 2, :], ALU.add)
        P.TT(P.dve, g2[:], g2[:], gp[:, 3, :], ALU.add)
        P.TT(P.dve, g2[:], g2[:], b2[:], ALU.add)
        ic = g2[:, 0:1]
        lf = sm["lf"].get()
        P.ACT(lf[:], g2[:, 1:2], AF.Exp, scale=-1.0)
        P.ACT(lf[:], lf[:], AF.Ln, bias=1.0)
        P.TS(P.dve, lf[:], lf[:], -1.0, None, ALU.mult)
        lf3 = r_lf3.get(); lft = r_c1t.get()
        split_bf16(P, lf[:], [lf3[:, i:i + 1] for i in range(3)], [lft[:, 0:1], lft[:, 1:2]])
        psV = psum.get()
        P.MM(psV[:, 0:3], C.U_b[:], lf3[:])
        bcol = sm["bcol"].get()
        P.RED(P.dve, bcol[:], psV[:, 0:3], ALU.add)
        ccol = sm["ccol"].get(); cb1 = r_cb.get(); cb2 = r_cb.get(); cr = r_c1t.get()
        P.TT(P.dve, ccol[:], ic, bcol[:], ALU.subtract)
        dg = [r.get() for r in r_dg]
        P.TS(P.dve, dg[0][:], C.ident_f[:], ccol[:], None, ALU.mult)
        P.CP(P.act, cb1[:], ccol[:])
        P.TT(P.dve, cr[:, 0:1], ccol[:], cb1[:], ALU.subtract)
        P.TS(P.dve, dg[1][:], C.ident_f[:], cr[:, 0:1], None, ALU.mult)
        P.CP(P.act, cb2[:], cr[:, 0:1])
        P.TT(P.dve, cr[:, 1:2], cr[:, 0:1], cb2[:], ALU.subtract)
        P.TS(P.dve, dg[2][:], C.ident_f[:], cr[:, 1:2], None, ALU.mult)
        psD = psum.get()
        for p_ in range(3):
            P.MM(psD[:, 0:128], C.ONES_b[:], dg[p_][:], start=(p_ == 0), stop=(p_ == 2))
        Dsb = r_R1.get()
        P.STT(P.dve, Dsb[:], psD[:, 0:128], bcol[:], C.NEGM[:], ALU.add, ALU.add)
        inter = sm["inter"].get(); rmax = sm["rmax"].get(); mi = sm["mi"].get(); nmi = sm["nmi"].get()
        P.TT(P.dve, inter[:], bcol[:], mrun[:], ALU.add)
        P.RED(P.dve, rmax[:], Dsb[:], ALU.max)
        P.TT(P.dve, mi[:], rmax[:], inter[:], ALU.max)
        P.TS(P.dve, nmi[:], mi[:], -1.0, None, ALU.mult)
        wi = r_wi.get(); winter = sm["winter"].get(); elim = sm["elim"].get()
        P.ACT(wi[:], Dsb[:], AF.Exp, bias=nmi[:])
        P.ACT(winter[:], inter[:], AF.Exp, bias=nmi[:])
        P.ACT(elim[:], nmi[:], AF.Exp)
        v2 = r_v2.get(); bc2 = r_bc2.get(); v6 = r_v6.get(); v2t = r_v2t.get(); bc6 = r_bc6.get()
        P.CP(P.dve, v2[:, 0:1], mi[:]); P.CP(P.dve, v2[:, 1:2], bcol[:])
        split_bf16(P, v2[:], [v6[:, i, :] for i in range(3)], [v2t[:, 0, :], v2t[:, 1, :]])
        psB = psum.get()
        P.MM(psB[:, 0:6], C.ELAST_b[:], v6[:].rearrange("p a b -> p (a b)"))
        P.CP(P.dve, bc6[:], psB[:, 0:6].rearrange("p (a b) -> p a b", a=3))
        P.TT(P.dve, bc2[:], bc6[:, 0, :], bc6[:, 1, :], ALU.add)
        P.TT(P.dve, bc2[:], bc2[:], bc6[:, 2, :], ALU.add)
        lw = sm["lw"].get(); nmn = sm["nmn"].get(); wj = sm["wj"].get(); carry = sm["carry"].get(); tcar = sm["tcar"].get()
        P.STT(P.dve, lw[:], bcol[:], -1.0, bc2[:, 1:2], ALU.mult, ALU.add)
        P.TT(P.dve, lw[:], lw[:], ic, ALU.add)
        P.TS(P.dve, nmn[:], bc2[:, 0:1], -1.0, None, ALU.mult)
        P.ACT(wj[:], lw[:], AF.Exp, bias=nmn[:])
        P.TS(P.dve, wj[:], wj[:], SC, None, ALU.mult)
        P.TT(P.dve, tcar[:], bc2[:, 1:2], mrun[:], ALU.add)
        P.ACT(carry[:], tcar[:], AF.Exp, bias=nmn[:])
        P.CP(P.dve, mrun[:], bc2[:, 0:1])
        psS = psum.get()
        for c in range(16):
            P.MM(psS[:, 0:128], q[:, c, :], k[:, c, :], start=(c == 0), stop=(c == 15))
        sc = r_sc.get(); rsum = sm["rsum"].get()
        P.STT(P.dve, sc[:], psS[:, 0:128], SC, wi[:], ALU.mult, ALU.mult, accum=rsum[:])
        pst = C.psb.get()
        P.TR(pst[:, 0:128], sc[:], C.ident_b[:])
        scT = r_scT.get()
        P.CP(P.act, scT[:], pst[:, 0:128])
        psQ = psum.get()
        for c in range(16):
            P.MM(psQ[:, 0:1], q[:, c, :], nvb[:, c:c + 1], start=(c == 0), stop=(c == 15))
        den = sm["den"].get(); rden = sm["rden"].get(); s2 = sm["s2"].get()
        P.STT(P.dve, den[:], psQ[:, 0:1], winter[:], rsum[:], ALU.mult, ALU.add)
        nden = sm["tcar"].get()
        P.TS(P.dve, nden[:], den[:], -1.0, None, ALU.mult)
        P.TT(P.dve, den[:], den[:], nden[:], ALU.max)
        P.TT(P.dve, den[:], den[:], elim[:], ALU.max)
        P.RECIP(rden[:], den[:])
        P.TT(P.dve, s2[:], winter[:], rden[:], ALU.mult)
        h32 = r_h.get()
        for s in range(4):
            vs = slice(s * 512, (s + 1) * 512)
            p1 = psum.get(); p2 = psum.get()
            P.MM(p1[:, :], scT[:], vt[:, vs])
            for c in range(16):
                P.MM(p2[:, :], q[:, c, :], Cb[:, c, vs], start=(c == 0), stop=(c == 15))
            t5 = r_t5.get()
            P.ACT(t5[:], p2[:, :], AF.Copy, scale=s2[:])
            P.STT(P.dve, h32[:, vs], p1[:, :], rden[:], t5[:], ALU.mult, ALU.add, acc=True)
        st = r_st.get(); mv = r_mv.get(); rs = r_rs.get()
        for c in range(4):
            P.op(P.dve, lambda hh, c=c: hh.bn_stats(out=st[:, c, :].ap, in_=h32[:, c * 512:(c + 1) * 512].ap), [h32[:]], [st[:]], acc=True)
        P.op(P.dve, lambda hh: hh.bn_aggr(out=mv[:].ap, in_=st[:].ap), [st[:]], [mv[:]])
        P.ACT(rs[:, 0:1], mv[:, 1:2], AF.Ln, bias=eps6[:])
        P.ACT(rs[:, 0:1], rs[:, 0:1], AF.Exp, scale=-0.5)
        P.STT(P.dve, rs[:, 1:2], mv[:, 0:1], -1.0, rs[:, 0:1], ALU.mult, ALU.mult)
        hn = r_hn.get()
        P.TS(P.dve, hn[:], h32[:], rs[:, 0:1], rs[:, 1:2], ALU.mult, ALU.add)
        gate = r_gate.get()
        silu_gate(P, gate[:], z[:], r_gt.get()[:])
        o1 = r_o1.get(); ob = r_ob.get()
        for c8 in range(2):
            pst = C.psb.get()
            for c in range(8):
                cc = c8 * 8 + c
                P.TR(pst[:, c * 128:(c + 1) * 128], hn[:, cc * 128:(cc + 1) * 128], C.ident_b[:])
            for c in range(8):
                cc = c8 * 8 + c
                P.STT(P.dve, o1[:, cc, :], pst[:, c * 128:(c + 1) * 128], ng[:, cc:cc + 1], sx[:, cc, :], ALU.mult, ALU.add, acc=True)
        P.TT(P.pool, ob[:], o1[:], gate[:], ALU.mult)
        P.dma(P.sp, out_x[:, :, ts], ob[:])
        kw = r_kw.get()
        P.TS(P.dve, kw[:], kt[:], wj[:], None, ALU.mult)
        for c in range(16):
            for s in range(4):
                vs = slice(s * 512, (s + 1) * 512)
                pu = psum.get()
                P.MM(pu[:, :], kw[:, c * 128:(c + 1) * 128], vt[:, vs])
                P.STT(P.dve, Cb[:, c, vs], Cb[:, c, vs], carry[:], pu[:, :], ALU.mult, ALU.add, acc=True)
        pn = psum.get()
        for c in range(16):
            P.MM(pn[:, 0:16], kw[:, c * 128:(c + 1) * 128], EC[:, c, :], start=(c == 0), stop=(c == 15))
        P.STT(P.dve, nv32[:], nv32[:], carry[:], pn[:, 0:16], ALU.mult, ALU.add)
        P.CP(P.dve, nvb[:], nv32[:])
    for c in range(0, 16, 4):
        P.dma(P.pool, io["C_out"][:, c:c + 4, :], Cb[:, c:c + 4, :], acc=True)
    P.dma(P.pool, io["nv_out"][:, :], nv32[:])
    P.dma(P.pool, io["m_out"][:, :], mrun[:])


def build_l1a2(T):
    nc, P, C, psum = new_prog()
    io = {}
    ext = lambda n, s, dt=F32: P.dram(n, s, dt, kind="ExternalInput")
    io["h1T"] = ext("h1T", [NCH1 * 128, T])
    io["qT"] = ext("qT", [2048, T], BF16)
    io["kT"] = ext("kT", [2048, T], BF16)
    io["sxcT"] = ext("sxcT", [2048, T], BF16)
    io["k_tm"] = ext("k_tm", [T, 2048], BF16)
    io["v_tm"] = ext("v_tm", [T, 2048], BF16)
    io["gp4"] = ext("gp4", [4, T, 2])
    io["b_if2"] = ext("b_if2", [1, 2])
    io["mh_ng"] = ext("mh_ng", [128, 16])
    io["memT"] = ext("memT", [D, 256])
    io["w_mk"] = ext("w_mk", [128, 32, 128])
    io["w_mv"] = ext("w_mv", [128, 32, 128])
    io["outT"] = P.dram("outT", [2176, T], BF16, kind="ExternalOutput")
    io["C_in"] = ext("C_in", [128, 16, 2048], BF16)
    io["nv_in"] = ext("nv_in", [128, 16])
    io["m_in"] = ext("m_in", [128, 1])
    io["C_out"] = P.dram("C_out", [128, 16, 2048], BF16, kind="ExternalOutput")
    io["nv_out"] = P.dram("nv_out", [128, 16], F32, kind="ExternalOutput")
    io["m_out"] = P.dram("m_out", [128, 1], F32, kind="ExternalOutput")
    phase_l1a2(P, C, T, io, psum)
    P.finish()
    return nc, P


def l1a2_inputs(inp, a1res, b, h):
    r = a1res[h]
    d = {k: r[k] for k in ("h1T", "qT", "kT", "sxcT", "k_tm", "v_tm")}
    d["gp4"] = np.ascontiguousarray(np.stack([a1res[g]["gp"][:, [h, 4 + h]] for g in range(4)], axis=0))
    d["b_if2"] = np.ascontiguousarray(inp["od_b_if"][0][[h, 4 + h]][None, :])
    d["mh_ng"] = np.ascontiguousarray(inp["od_mh_norm_g"][0][h * 2048:(h + 1) * 2048].reshape(16, 128).T)
    d["memT"] = np.ascontiguousarray(inp["mem"][b].T)
    tk = lambda m: np.ascontiguousarray(m[:, h * 128:(h + 1) * 128].reshape(32, 128, 128).transpose(1, 0, 2))
    d["w_mk"] = tk(inp["od_mem_w_k"][0])
    d["w_mv"] = tk(inp["od_mem_w_v"][0])
    return d


def _run(nc, maps):
    return run_bass_kernel_spmd(nc, maps, core_ids=list(range(NCORES))).results


def kernel(**inp):
    inp = {k: np.asarray(v) for k, v in inp.items()}
    B, T, _ = inp["x"].shape
    TS = min(T, SEG_TOKENS)
    NS = T // TS
    TB = TS // 4
    cores = [(b, g) for b in range(2) for g in range(4)]
    bf = ml_dtypes.bfloat16
    w0 = inp["ev_w_out"][0]
    perm0 = np.concatenate([np.r_[g * 512:(g + 1) * 512, 2048 + g * 512:2048 + (g + 1) * 512,
                                  4096 + g * 128:4096 + (g + 1) * 128] for g in range(4)])
    wl0 = tile_w_tm(w0[perm0])
    w1 = inp["od_w_out"][0]
    perm1 = np.concatenate([np.r_[h * 2048:(h + 1) * 2048, 8192 + h * 128:8192 + (h + 1) * 128] for h in range(4)])
    wl1 = tile_w_tm(w1[perm1])
    nc0a = [build_l0a(TS, None, debug=False, precast=p_)[0] for p_ in (False, True)] if NS > 1 else [build_l0a(TS, None, debug=False)[0]]
    ncb0 = [build_b(TB, 4608, True, precast=p_)[0] for p_ in (False, True)] if NS > 1 else [build_b(TB, 4608, True)[0]]
    nc1a1 = [build_l1a1(TS, precast=p_)[0] for p_ in (False, True)] if NS > 1 else [build_l1a1(TS)[0]]
    nc1a2, _ = build_l1a2(TS)
    ncb1 = [build_b(TB, 8704, False, precast=p_)[0] for p_ in (False, True)] if NS > 1 else [build_b(TB, 8704, False)[0]]
    wb = {}
    base0 = [l0a_inputs(inp, b, g, TS) for b, g in cores]
    st0 = [{"gS_in": np.zeros((128, 2, 512), np.float32), "hS_in": np.zeros((4, 128, 1, 128), np.float32)} for _ in cores]
    halo = [np.zeros((128, 16, 3), np.float32) for _ in cores]
    st1 = [{"C_in": np.zeros((128, 16, 2048), bf), "nv_in": np.zeros((128, 16), np.float32),
            "m_in": np.zeros((128, 1), np.float32)} for _ in cores]
    out = np.empty((B, T, D), np.float32)
    for s_ in range(NS):
        seg = slice(s_ * TS, (s_ + 1) * TS)
        maps = []
        for ci, (b, g) in enumerate(cores):
            m = dict(base0[ci])
            m["xT"] = np.ascontiguousarray(inp["x"][b, seg].T)
            m.update(st0[ci])
            if s_ > 0:
                del m["w_fm"], m["w_tm"]
                m["w_fmb"], m["w_tmb"] = wb["l0a"][ci]
            maps.append(m)
        r0 = _run(nc0a[min(s_, len(nc0a) - 1)], maps)
        if s_ == 0:
            wb["l0a"] = [(np.asarray(r["w_fmb"]), np.asarray(r["w_tmb"])) for r in r0]
        out0 = [np.asarray(r["outT"]) for r in r0]
        st0 = [{"gS_in": np.asarray(r["gS_out"]), "hS_in": np.asarray(r["hS_out"])} for r in r0]
        del r0
        maps = []
        for b, g in cores:
            sl = slice(g * TB, (g + 1) * TB)
            maps.append({"catT": np.ascontiguousarray(np.concatenate([out0[b * 4 + gg][:, sl] for gg in range(4)], axis=0)),
                         "w_out": wl0, "xres": np.ascontiguousarray(inp["x"][b, seg][sl]),
                         "ln_g": inp["ev_ln_g"], "ln_b": inp["ev_ln_b"]})
        if s_ > 0:
            for ci, m in enumerate(maps):
                del m["w_out"]
                m["w_outb"] = wb["b0"][ci]
        rb0 = _run(ncb0[min(s_, len(ncb0) - 1)], maps)
        if s_ == 0:
            wb["b0"] = [np.asarray(r["w_outb"]) for r in rb0]
        x1 = [np.asarray(r["xo"]) for r in rb0]
        x1T = [np.ascontiguousarray(np.concatenate([np.asarray(rb0[b * 4 + g]["xoT"]) for g in range(4)], axis=1)) for b in range(2)]
        del rb0, out0
        maps = []
        for ci, (b, h) in enumerate(cores):
            m = l1a1_inputs(inp, x1T[b], h, TS)
            m["halo_in"] = halo[ci]
            if s_ > 0:
                del m["w_fm"]
                m["w_fmb"] = wb["l1a1"][ci]
            maps.append(m)
        r1 = _run(nc1a1[min(s_, len(nc1a1) - 1)], maps)
        if s_ == 0:
            wb["l1a1"] = [np.asarray(r["w_fmb"]) for r in r1]
        halo = [np.asarray(r["halo_out"]) for r in r1]
        maps = []
        for ci, (b, h) in enumerate(cores):
            m = l1a2_inputs(inp, r1[b * 4:(b + 1) * 4], b, h)
            m.update(st1[ci])
            maps.append(m)
        r2 = _run(nc1a2, maps)
        out1 = [np.asarray(r["outT"]) for r in r2]
        st1 = [{"C_in": np.asarray(r["C_out"]), "nv_in": np.asarray(r["nv_out"]), "m_in": np.asarray(r["m_out"])} for r in r2]
        del r1, r2
        maps = []
        for b, g in cores:
            sl = slice(g * TB, (g + 1) * TB)
            maps.append({"catT": np.ascontiguousarray(np.concatenate([out1[b * 4 + hh][:, sl] for hh in range(4)], axis=0)),
                         "w_out": wl1, "xres": x1[b * 4 + g], "ln_g": inp["od_ln_g"], "ln_b": inp["od_ln_b"]})
        if s_ > 0:
            for ci, m in enumerate(maps):
                del m["w_out"]
                m["w_outb"] = wb["b1"][ci]
        rb1 = _run(ncb1[min(s_, len(ncb1) - 1)], maps)
        if s_ == 0:
            wb["b1"] = [np.asarray(r["w_outb"]) for r in rb1]
        for ci, (b, g) in enumerate(cores):
            out[b, s_ * TS + g * TB:s_ * TS + (g + 1) * TB] = np.asarray(rb1[ci]["xo"])
        del rb1, out1
    return out
```
